# Optimizing a Trainium2 kernel written in Bass

```python
import jax, jax.numpy as jnp
from jax import lax
import numpy as np

D_MODEL = 1024
BATCH = 8
SEQ = 2048
DEPTH = 1
DEC_BATCH = 128
DEC_SEQ = 1
PAST_LEN = 16384
PAGE_SIZE = 128

EXPAND = 2
D_MIX = EXPAND * D_MODEL
W_A = D_MIX // 2
W_B = D_MIX - W_A
CONV_WIDTH = 31
B_HEADS = 8
B_HEAD_DIM = W_B // B_HEADS
CHUNK = 128
D_IN = 3 * W_A + 3 * W_B
EPS = 1e-6

kernel_name = "hybrid_conv_chunkgmlp_decode_step"


def _rmsnorm(x, g):
    xf = x.astype(jnp.float32)
    y = xf * lax.rsqrt(jnp.mean(xf * xf, axis=-1, keepdims=True) + EPS)
    return (y * g.astype(jnp.float32)).astype(x.dtype)


def _layernorm(x, g, b):
    xf = x.astype(jnp.float32)
    mu = jnp.mean(xf, axis=-1, keepdims=True)
    var = jnp.mean(jnp.square(xf - mu), axis=-1, keepdims=True)
    y = (xf - mu) * lax.rsqrt(var + EPS)
    return (y * g.astype(jnp.float32) + b.astype(jnp.float32)).astype(x.dtype)


def _depthwise_conv_valid(x_padded, w, b):
    c = x_padded.shape[-1]
    out = lax.conv_general_dilated(
        x_padded, w[:, None, :].astype(x_padded.dtype), window_strides=(1,), padding="VALID",
        dimension_numbers=("NWC", "WIO", "NWC"), feature_group_count=c)
    return out + b


def _spatial_gate(v, w_s, b_s):
    L = v.shape[2]
    mask = jnp.tril(jnp.ones((L, L), dtype=bool))
    ws = jnp.where(mask[None], w_s[:, :L, :L], jnp.zeros((), w_s.dtype))
    out = jnp.einsum("hts,ncshd->ncthd", ws, v)
    return out + jnp.transpose(b_s[:, :L])[None, None, :, :, None]


def _layer(x, conv_buf, chunk_len, norm_g, w_in, conv_w, conv_b, a_ln_g, a_ln_b,
           b_ln_g, b_ln_b, w_s, b_s, w_out):
    n, t, _ = x.shape
    h = _rmsnorm(x, norm_g)
    z = h @ w_in
    a_val, a_gl, a_gate, b_u, b_v, b_gate = jnp.split(
        z, [W_A, 2 * W_A, 3 * W_A, 3 * W_A + W_B, 3 * W_A + 2 * W_B], axis=-1)
    a_in = a_val * jax.nn.sigmoid(a_gl)
    a_seq = jnp.concatenate([conv_buf, a_in], axis=1)
    a_conv = _depthwise_conv_valid(a_seq, conv_w, conv_b)
    a_out = jax.nn.silu(_layernorm(a_conv, a_ln_g, a_ln_b)) * jax.nn.silu(a_gate)
    new_buf = a_seq[:, -(CONV_WIDTH - 1):]
    vn = _layernorm(b_v, b_ln_g, b_ln_b)
    vc = vn.reshape(n, t // chunk_len, chunk_len, B_HEADS, B_HEAD_DIM)
    gated = _spatial_gate(vc, w_s, b_s).reshape(n, t, W_B)
    b_out = b_u * gated * jax.nn.silu(b_gate)
    y = x + jnp.concatenate([a_out, b_out], axis=-1) @ w_out
    return y, new_buf, b_v[:, -chunk_len:]


def setup_inputs(seed: int = 0) -> dict:
    key = jax.random.key(seed)
    ks = jax.random.split(key, 16)
    f32 = jnp.float32
    x_prompt = jax.random.normal(ks[0], (BATCH, SEQ, D_MODEL), f32)
    x_sample = jax.random.normal(ks[1], (DEC_BATCH, DEC_SEQ, D_MODEL), f32)
    state_conv = 0.5 * jax.random.normal(ks[2], (DEPTH, DEC_BATCH, CONV_WIDTH - 1, W_A), f32)
    norm_g = 1.0 + 0.01 * jax.random.normal(ks[3], (DEPTH, D_MODEL), f32)
    w_in = jax.random.normal(ks[4], (DEPTH, D_MODEL, D_IN), f32) * D_MODEL ** -0.5
    conv_w = jax.random.normal(ks[5], (DEPTH, CONV_WIDTH, W_A), f32) * CONV_WIDTH ** -0.5
    conv_b = 0.01 * jax.random.normal(ks[6], (DEPTH, W_A), f32)
    a_ln_g = 1.0 + 0.01 * jax.random.normal(ks[7], (DEPTH, W_A), f32)
    a_ln_b = 0.01 * jax.random.normal(ks[8], (DEPTH, W_A), f32)
    b_ln_g = 1.0 + 0.01 * jax.random.normal(ks[9], (DEPTH, W_B), f32)
    b_ln_b = 0.01 * jax.random.normal(ks[10], (DEPTH, W_B), f32)
    w_s = jax.random.normal(ks[11], (DEPTH, B_HEADS, CHUNK, CHUNK), f32) * CHUNK ** -0.5
    b_s = 1.0 + 0.01 * jax.random.normal(ks[12], (DEPTH, B_HEADS, CHUNK), f32)
    w_out = jax.random.normal(ks[13], (DEPTH, D_MIX, D_MODEL), f32) * (0.5 * D_MIX ** -0.5)
    final_g = 1.0 + 0.01 * jax.random.normal(ks[14], (D_MODEL,), f32)
    return {"x_prompt": x_prompt, "x_sample": x_sample, "state_conv": state_conv,
            "norm_g": norm_g, "w_in": w_in, "conv_w": conv_w, "conv_b": conv_b,
            "a_ln_g": a_ln_g, "a_ln_b": a_ln_b, "b_ln_g": b_ln_g, "b_ln_b": b_ln_b,
            "w_s": w_s, "b_s": b_s, "w_out": w_out, "final_g": final_g}


def reference(x_prompt, x_sample, state_conv, norm_g, w_in, conv_w, conv_b, a_ln_g, a_ln_b,
              b_ln_g, b_ln_b, w_s, b_s, w_out, final_g):
    hp = x_prompt
    hs = x_sample
    conv_p, conv_s, v_p, v_s = [], [], [], []
    for l in range(DEPTH):
        params = (norm_g[l], w_in[l], conv_w[l], conv_b[l], a_ln_g[l], a_ln_b[l],
                  b_ln_g[l], b_ln_b[l], w_s[l], b_s[l], w_out[l])
        zero_buf = jnp.zeros((hp.shape[0], CONV_WIDTH - 1, W_A), hp.dtype)
        hp, cb_p, vr_p = _layer(hp, zero_buf, CHUNK, *params)
        hs, cb_s, vr_s = _layer(hs, state_conv[l].astype(hs.dtype), hs.shape[1], *params)
        conv_p.append(cb_p)
        conv_s.append(cb_s)
        v_p.append(vr_p)
        v_s.append(vr_s)
    y_prompt = _rmsnorm(hp, final_g)
    y_sample = _rmsnorm(hs, final_g)
    conv_state_prompt = jnp.stack(conv_p, axis=0)
    conv_state_sample = jnp.stack(conv_s, axis=0)
    chunk_v_prompt = jnp.stack(v_p, axis=0)
    chunk_v_sample = jnp.stack(v_s, axis=0)
    return (y_prompt, y_sample, conv_state_prompt, conv_state_sample, chunk_v_prompt, chunk_v_sample)
```

```python
import numpy as np
from contextlib import ExitStack
import concourse.bass as bass
import concourse.mybir as mybir
from concourse.bass_utils import run_bass_kernel_spmd

F32 = mybir.dt.float32
BF16 = mybir.dt.bfloat16
AF = mybir.ActivationFunctionType
ALU = mybir.AluOpType

D = 1024
DIN = 6144
CW = 31
EPS = 1e-6
NB = 256
TD = 7
N_CORES = 8


class Res:
    __slots__ = ("name", "last_writer", "readers")

    def __init__(self, name=""):
        self.name = name
        self.last_writer = None
        self.readers = []


class Op:
    __slots__ = ("eng", "fn", "idx", "waits", "signal", "semval", "dma_sem", "dma_val", "vc", "dwait")


class Sched:
    ENGS = ("pe", "act", "dve", "pool", "sp")

    def __init__(self, nc):
        self.nc = nc
        self.ops = {e: [] for e in self.ENGS}
        self.all_ops = []
        self.dma_cnt = {}
        self.know = {e: {} for e in self.ENGS}

    def op(self, eng, fn, reads=(), writes=(), excl=(), dma=None):
        o = Op()
        o.eng = eng
        o.fn = fn
        o.idx = len(self.ops[eng]) + 1
        o.signal = False
        o.semval = None
        o.dma_sem = dma
        o.dma_val = None
        o.dwait = {}
        deps = []
        for r in reads:
            if r.last_writer is not None:
                deps.append(r.last_writer)
        for w in writes:
            if w.last_writer is not None:
                deps.append(w.last_writer)
            deps.extend(w.readers)
        for x in excl:
            if x.last_writer is not None and x.last_writer.eng != eng:
                deps.append(x.last_writer)
        for x in excl:
            x.last_writer = o
            x.readers = []
        for r in reads:
            r.readers.append(o)
        for w in writes:
            w.last_writer = o
            w.readers = []
        K = self.know[eng]
        waits = []

        def clk(d):
            return ("dma", d.dma_sem) if d.dma_sem is not None else ("eng", d.eng)

        def cval(d):
            return self.dma_cnt[d.dma_sem] if d.dma_sem is not None else d.idx

        seen = set()
        for d in sorted(deps, key=lambda d: -cval(d)):
            if id(d) in seen or d is o:
                continue
            seen.add(id(d))
            c = clk(d)
            if c == ("eng", "pe") and eng == "pe":
                continue
            if K.get(c, 0) >= cval(d):
                continue
            waits.append(d)
            if d.dma_sem is not None:
                o.dwait[id(d)] = cval(d)
            for k, v in d.vc.items():
                if K.get(k, 0) < v:
                    K[k] = v
            K[c] = cval(d)
        o.waits = waits
        o.vc = dict(K)
        if dma is not None:
            self.dma_cnt[dma] = self.dma_cnt.get(dma, 0) + 1
            o.dma_val = self.dma_cnt[dma]
        self.ops[eng].append(o)
        self.all_ops.append(o)
        return o

    def run(self):
        nc = self.nc
        for o in self.all_ops:
            for d in o.waits:
                if d.dma_sem is None:
                    d.signal = True
        for e in self.ENGS:
            c = 0
            for o in self.ops[e]:
                if o.signal:
                    c += 1
                    o.semval = c
        with ExitStack() as es:
            esem = {e: es.enter_context(nc.semaphore("s_" + e)) for e in self.ENGS}
            dsem = {k: es.enter_context(nc.semaphore("d_" + str(k))) for k in self.dma_cnt}
            block = es.enter_context(nc.Block())

            def emit(e, eng):
                for o in self.ops[e]:
                    for d in o.waits:
                        if d.dma_sem is not None:
                            eng.wait_ge(dsem[d.dma_sem], 16 * o.dwait[id(d)])
                        else:
                            eng.wait_ge(esem[d.eng], d.semval)
                    if o.fn is None:
                        continue
                    ins = o.fn(eng)
                    if o.dma_sem is not None:
                        ins.then_inc(dsem[o.dma_sem], 16)
                    elif o.signal:
                        ins.then_inc(esem[e], 1)

            @block.tensor
            def _(eng):
                emit("pe", eng)

            @block.scalar
            def _(eng):
                emit("act", eng)

            @block.vector
            def _(eng):
                emit("dve", eng)

            @block.gpsimd
            def _(eng):
                emit("pool", eng)

            @block.sync
            def _(eng):
                emit("sp", eng)


def build_program(SEQ=2048, NS=16):
    assert SEQ % NB == 0 and NS == 16
    NBLK = SEQ // NB
    nc = bass.Bass("TRN2", target_bir_lowering=False)

    def din(name, shape):
        return nc.dram_tensor(name, list(shape), F32, kind="ExternalInput").ap()

    def dout(name, shape):
        return nc.dram_tensor(name, list(shape), F32, kind="ExternalOutput").ap()

    x_d = din("x", [SEQ, D])
    xs_d = din("xs", [NS, D])
    st_d = din("st", [NS * 30, D])
    win_d = din("w_in", [D, DIN])
    wout_d = din("w_out", [2 * D, D])
    ng_d = din("ng", [128, 8])
    cw_d = din("cw", [128, 8 * CW])
    cb_d = din("cb", [128, 8])
    alg_d = din("alg", [128, 8])
    alb_d = din("alb", [128, 8])
    blg_d = din("blg", [128, 8])
    blb_d = din("blb", [128, 8])
    ws_d = din("ws", [128, 8 * 128])
    bsB_d = din("bsB", [128, 8 * 128])
    fgB_d = din("fgB", [128, D])
    ws00_d = din("ws00", [128, 8])
    bs0_d = din("bs0", [128, 8])
    cw30_d = din("cw30", [120, D])
    cb8_d = din("cb8", [8, 128])

    y_d = dout("y", [SEQ, D])
    ys_d = dout("ys", [NS, D])
    csp_d = dout("csp", [30, D])
    css_d = dout("css", [NS * 30, D])
    cvp_d = dout("cvp", [128, D])
    cvs_d = dout("cvs", [NS, D])

    S = Sched(nc)
    es = ExitStack()
    with es:
        def sb(name, shape, dt=F32):
            return es.enter_context(nc.sbuf_tensor("s_" + name, list(shape), dt))

        W = sb("W", [128, 8, DIN], BF16)
        WO = sb("WO", [128, 16, D], BF16)
        rW = [Res("W%d" % i) for i in range(12)]
        rWO = [Res("WO%d" % i) for i in range(4)]
        ident = sb("ident", [128, 128])
        onesf = sb("onesf", [128, 128])
        onesb = sb("onesb", [128, 128], BF16)
        ng = sb("ng", [128, 8])
        wh = sb("wh", [128, 8 * CW])
        cb = sb("cb", [128, 8])
        alg = sb("alg", [128, 8])
        alb = sb("alb", [128, 8])
        blg = sb("blg", [128, 8])
        blb = sb("blb", [128, 8])
        wsT = sb("wsT", [128, 8, 128], BF16)
        Cm = sb("Cm", [128, 8, 128])
        fgB = sb("fgB", [128, D])
        coefA = sb("coefA", [128, 8])
        coefB = sb("coefB", [128, 8])
        ws00 = sb("ws00", [128, 8])
        bs0 = sb("bs0", [128, 8])
        mhalf = sb("mhalf", [128, 4])
        sel = sb("sel", [128, 4], BF16)
        self32 = sb("self32", [128, 4])
        identb = sb("identb", [128, 128], BF16)
        cb8 = sb("cb8", [8, 128], BF16)
        TPE = CW - TD
        TS = 8
        NSLOT = 4
        dslot = sb("dslot", [128, NSLOT, TS, 128], BF16)
        rslot = [Res("dslot%d" % i) for i in range(NSLOT)]
        slotctr = [0]
        dg = nc.dram_tensor("dg_scratch", [8, 128, (CW - TD) * 128], BF16, kind="ExternalOutput").ap()
        rdg = [Res("dg%d" % c) for c in range(8)]
        rC = {k: Res(k) for k in ["ident", "onesf", "onesb", "ng", "wh", "cb", "alg", "alb", "blg", "blb",
                                  "wsT", "Cm", "fgB", "coefA", "coefB", "ws00", "bs0", "mhalf", "sel",
                                  "self32", "identb", "cb8"]}
        xf = sb("xf", [128, D])
        rxf = Res("xf")
        xh = sb("xh", [128, D])
        rxh = Res("xh")
        vraw = xh
        rvraw = rxh
        xT = sb("xT", [128, 8, NB], BF16)
        rxT = Res("xT")
        a2 = sb("a2", [128, 8, 30 + NB], BF16)
        ra2 = [Res("a2_%d" % c) for c in range(8)]
        th0 = sb("th0", [128, NB])
        th = [th0, th0]
        rth0 = Res("th0")
        rth = [rth0, rth0]
        acc = [sb("acc%d" % i, [128, NB]) for i in range(2)]
        racc = [Res("acc%d" % i) for i in range(2)]
        acv = sb("acv", [128, 8, NB], BF16)
        racv = [Res("acv%d" % c) for c in range(8)]
        sq = [sb("sq%d" % i, [128, NB], BF16) for i in range(2)]
        rsq = [Res("sq%d" % i) for i in range(2)]
        sg2 = [sb("sg_%d" % i, [128, 8, NB], BF16) for i in range(2)]
        rsg2 = [[Res("sg%d_%d" % (i, c)) for c in range(4)] for i in range(2)]
        sgb2 = [sb("sgb%d" % i, [128, NB]) for i in range(2)]
        rsgb2 = [Res("sgb%d" % i) for i in range(2)]
        ug2 = [sb("ug%d" % i, [128, NB]) for i in range(2)]
        rug2 = [Res("ug%d" % i) for i in range(2)]
        tmpb2 = sb("tmpb", [128, 2, NB])
        tmpb = [tmpb2[:, 0, :], tmpb2[:, 1, :]]
        rtb = Res("tmpb0")
        rtb1 = Res("tmpb1")
        rtmpb = [rtb, rtb1]
        diag = tmpb2[:].rearrange("p a (b c) -> p (a b) c", c=128)
        rdiag = rtb
        nrm = sb("nrm", [128, 2, D], BF16)
        rnrm = [Res("nrm%d" % i) for i in range(2)]
        mixT = sb("mixT", [128, 16, NB], BF16)
        rmix = [Res("mix%d" % c) for c in range(16)]
        tmpa = sb("tmpa", [128, NB])
        rtmpa = Res("tmpa")
        a2f = tmpa[:].rearrange("p (c j) -> p c j", c=8)
        ra2f = rtmpa
        yb0 = sb("yb0", [128, D])
        yb = [yb0, yb0]
        ryb0 = Res("yb0")
        ryb = [ryb0, ryb0]
        small = sb("small", [128, 64])
        rsm = {}
        xT_s = sb("xT_s", [128, 8, NS], BF16)
        a2_s = sb("a2_s", [128, 8, 30 + NS], BF16)
        acv_s = sb("acv_s", [128, 8, NS], BF16)
        sg_s = sb("sg_s", [128, 8, NS], BF16)
        mixT_s = sb("mixT_s", [128, 16, NS], BF16)
        cx_prompt = dict(xT=xT, rxT=rxT, a2=a2, ra2=ra2, acv=acv, racv=racv, sg2=sg2, rsg2=rsg2, mixT=mixT, rmix=rmix)
        rsg_s = [Res("sgs%d" % c) for c in range(4)]
        cx_sample = dict(xT=xT_s, rxT=Res("xT_s"), a2=a2_s, ra2=[Res("a2s%d" % c) for c in range(8)],
                         acv=acv_s, racv=[Res("acvs%d" % c) for c in range(8)],
                         sg2=[sg_s, sg_s], rsg2=[rsg_s, rsg_s], mixT=mixT_s, rmix=[Res("mixs%d" % c) for c in range(16)])
        cx = dict(cx_prompt)

        def smr(k):
            if k not in rsm:
                rsm[k] = Res("sm" + k)
            return rsm[k]

        SS, MS, RSTD = 0, 2, 4
        ST6 = 8
        MV = 20
        VR, VN = 22, 23
        AS = 24
        YS = 40

        banks = [es.enter_context(nc.psum_tensor("pb%d" % i, [128, 512], F32)) for i in range(8)]
        rbank = [Res("pb%d" % i) for i in range(8)]
        bctr = [0]

        def nb():
            i = bctr[0] % 8
            bctr[0] += 1
            return banks[i], rbank[i]

        out_res = []

        def dma_out(dst, src, rsrc, key):
            r = Res("o")
            S.op("sp", lambda e: e.dma_start(out=dst, in_=src), reads=[rsrc], writes=[r], dma=key)
            out_res.append(r)

        def dma_in(dst, src, res, key, eng="sp"):
            S.op(eng, lambda e: e.dma_start(out=dst, in_=src), writes=[res], dma=key)

        def load_x(row0, nrows=128, src=None):
            src = x_d if src is None else src
            S.op("sp", lambda e: e.dma_start(out=xf[0:nrows, :], in_=src[row0:row0 + nrows, :]),
                 writes=[rxf], dma="xf")

        load_x(0)
        dma_in(ng[:], ng_d, rC["ng"], "c0")
        dma_in(wh[:], cw_d, rC["wh"], "c0")
        dma_in(cb[:], cb_d, rC["cb"], "c0")
        dma_in(alg[:], alg_d, rC["alg"], "c0")
        dma_in(alb[:], alb_d, rC["alb"], "c0")
        dma_in(blg[:], blg_d, rC["blg"], "c0")
        dma_in(blb[:], blb_d, rC["blb"], "c0")
        dma_in(ws00[:], ws00_d, rC["ws00"], "c0")
        dma_in(bs0[:], bs0_d, rC["bs0"], "c0")
        dma_in(fgB[:], fgB_d, rC["fgB"], "c1")
        dma_in(yb[0][:], ws_d, ryb[0], "c1")
        dma_in(Cm[:].rearrange("p h t -> p (h t)"), bsB_d, rC["Cm"], "c1")

        w_order = [2, 3, 0, 1, 4, 5, 8, 9, 10, 11, 6, 7]
        for cbk in w_order:
            S.op("pool", lambda e, cbk=cbk: e.dma_start(
                out=W[:, :, cbk * 512:(cbk + 1) * 512],
                in_=win_d[:, cbk * 512:(cbk + 1) * 512].rearrange("(kc p) n -> p kc n", p=128)),
                writes=[rW[cbk]], dma="W%d" % cbk)
            if cbk == 2:
                S.op("pool", lambda e: e.memset(ident[:], 0.0), writes=[rC["ident"]])
                S.op("pool", lambda e: e.affine_select(out=ident[:], in_=ident[:], pattern=[[-1, 128]],
                                                       compare_op=ALU.not_equal, fill=1.0, base=0,
                                                       channel_multiplier=1),
                     reads=[rC["ident"]], writes=[rC["ident"]])
                S.op("pool", lambda e: e.tensor_copy(out=identb[:], in_=ident[:]), reads=[rC["ident"]],
                     writes=[rC["identb"]])
                S.op("pool", lambda e: e.memset(mhalf[:], -0.5), writes=[rC["mhalf"]])
                S.op("pool", lambda e: e.memset(onesf[:], 1.0), writes=[rC["onesf"]])
                S.op("pool", lambda e: e.memset(onesb[:], 1.0), writes=[rC["onesb"]])
                for c in range(8):
                    S.op("pool", lambda e, c=c: e.memset(a2[:, c, 0:30], 0.0), writes=[ra2[c]])
        S.op("pool", lambda e: e.dma_start(out=cb8[:], in_=cb8_d), writes=[rC["cb8"]], dma="cb8")
        def load_wout():
            for q in range(4):
                S.op("pool", lambda e, q=q: e.dma_start(
                    out=WO[:, q * 4:(q + 1) * 4, :],
                    in_=wout_d[q * 512:(q + 1) * 512, :].rearrange("(kc p) n -> p kc n", p=128)),
                    writes=[rWO[q]], dma="WO%d" % q)

        S.op("dve", lambda e: e.tensor_scalar(out=wh[:], in0=wh[:], scalar1=0.5, scalar2=None, op0=ALU.mult),
             reads=[rC["wh"]], writes=[rC["wh"]])
        wsn = yb[0][:].rearrange("p (h s) -> p h s", h=8)
        S.op("pool", lambda e: e.affine_select(out=wsn, in_=wsn, pattern=[[0, 8], [-1, 128]],
                                               compare_op=ALU.is_ge, fill=0.0, base=0, channel_multiplier=1),
             reads=[ryb[0]], writes=[ryb[0]])
        for half in range(2):
            bk, rb = nb()
            for j in range(4):
                h = half * 4 + j
                S.op("pe", lambda e, bk=bk, j=j, h=h: e.transpose(out=bk[:, j * 128:(j + 1) * 128],
                                                                 in_=yb[0][:, h * 128:(h + 1) * 128],
                                                                 identity=ident[:]),
                     reads=[ryb[0], rC["ident"]], excl=[rb])
            S.op("act", lambda e, bk=bk, half=half: e.activation(
                out=wsT[:, half * 4:(half + 1) * 4, :].rearrange("p h t -> p (h t)"), in_=bk[:], func=AF.Copy),
                excl=[rb], writes=[rC["wsT"]])
        for half in range(2):
            bk, rb = nb()
            S.op("pe", lambda e, bk=bk, half=half: e.matmul(
                out=bk[:], lhsT=onesb[:, 0:128], rhs=wsT[:, half * 4:(half + 1) * 4, :].rearrange("p h t -> p (h t)"),
                start=True, stop=True), reads=[rC["onesb"], rC["wsT"]], excl=[rb])
            for j in range(4):
                h = half * 4 + j
                S.op("dve", lambda e, bk=bk, j=j, h=h: e.scalar_tensor_tensor(
                    out=Cm[:, h, :], in0=bk[:, j * 128:(j + 1) * 128], scalar=blb[:, h:h + 1],
                    in1=Cm[:, h, :], op0=ALU.mult, op1=ALU.add),
                    reads=[rC["blb"], rC["Cm"]], excl=[rb], writes=[rC["Cm"]])
        S.op("dve", lambda e: e.tensor_tensor(out=coefA[:], in0=ws00[:], in1=blg[:], op=ALU.mult),
             reads=[rC["ws00"], rC["blg"]], writes=[rC["coefA"]])
        S.op("dve", lambda e: e.tensor_tensor(out=coefB[:], in0=ws00[:], in1=blb[:], op=ALU.mult),
             reads=[rC["ws00"], rC["blb"]], writes=[rC["coefB"]])
        S.op("dve", lambda e: e.tensor_tensor(out=coefB[:], in0=coefB[:], in1=bs0[:], op=ALU.add),
             reads=[rC["coefB"], rC["bs0"]], writes=[rC["coefB"]])
        S.op("pool", lambda e: e.memset(self32[:], 1.0), writes=[rC["self32"]])
        S.op("pool", lambda e: e.affine_select(out=self32[:], in_=self32[:], pattern=[[-30, 4]],
                                               compare_op=ALU.is_ge, fill=0.0, base=0, channel_multiplier=1),
             reads=[rC["self32"]], writes=[rC["self32"]])
        S.op("pool", lambda e: e.affine_select(out=self32[:], in_=self32[:], pattern=[[30, 4]],
                                               compare_op=ALU.is_gt, fill=0.0, base=30, channel_multiplier=-1),
             reads=[rC["self32"]], writes=[rC["self32"]])
        S.op("pool", lambda e: e.tensor_copy(out=sel[:], in_=self32[:]), reads=[rC["self32"]], writes=[rC["sel"]])

        def pool_pow(col_in, col_out, n, P, rin, rout):
            S.op("pool", lambda e: e.tensor_tensor(out=small[0:P, col_out:col_out + n],
                                                   in0=small[0:P, col_in:col_in + n],
                                                   in1=mhalf[0:P, 0:n], op=ALU.pow),
                 reads=[rin, rC["mhalf"]], writes=[rout])

        def front_end_stats(P=128, junk=None, rjunk=None):
            if junk is None:
                junk, rjunk = xh, [rxh]
            S.op("act", lambda e: e.activation(out=junk[0:P, 0:D], in_=xf[0:P, :], func=AF.Square,
                                               accum_out=small[0:P, SS:SS + 1]),
                 reads=[rxf], writes=[smr("ss")] + list(rjunk))
            S.op("dve", lambda e: e.tensor_scalar(out=small[0:P, MS:MS + 1], in0=small[0:P, SS:SS + 1],
                                                  scalar1=1.0 / D, scalar2=EPS, op0=ALU.mult, op1=ALU.add),
                 reads=[smr("ss")], writes=[smr("ms")])
            pool_pow(MS, RSTD, 1, P, smr("ms"), smr("rstd"))

        def front_end_copy(P=128):
            S.op("act", lambda e: e.activation(out=xh[0:P, :], in_=xf[0:P, :], func=AF.Copy,
                                               scale=small[0:P, RSTD:RSTD + 1]),
                 reads=[rxf, smr("rstd")], writes=[rxh])

        def front_end_a(P=128):
            front_end_stats(P)
            front_end_copy(P)

        def front_end_b(tcol, P=128, ntok=128):
            xT = cx["xT"]
            rxT = cx["rxT"]
            for half in range(2):
                bk, rb = nb()
                for j in range(4):
                    k = half * 4 + j
                    S.op("pe", lambda e, bk=bk, j=j, k=k: e.transpose(
                        out=bk[:, j * 128:j * 128 + P], in_=xh[0:P, k * 128:(k + 1) * 128],
                        identity=ident[0:P, 0:P]),
                        reads=[rxh, rC["ident"]], excl=[rb])
                for j in range(4):
                    k = half * 4 + j
                    S.op("act", lambda e, bk=bk, j=j, k=k: e.activation(
                        out=xT[:, k, tcol:tcol + ntok], in_=bk[:, j * 128:j * 128 + ntok], func=AF.Copy,
                        scale=ng[:, k:k + 1]),
                        reads=[rC["ng"]], excl=[rb], writes=[rxT])

        def front_end(tcol, P=128, ntok=128):
            front_end_a(P)
            front_end_b(tcol, P, ntok)

        def zT_group(bk, col0, fcol, n, rb):
            xT = cx["xT"]
            rxT = cx["rxT"]
            for k in range(8):
                S.op("pe", lambda e, k=k: e.matmul(out=bk[:, col0:col0 + n], lhsT=W[:, k, fcol:fcol + 128],
                                                   rhs=xT[:, k, 0:n], start=(k == 0), stop=(k == 7)),
                     reads=[rxT, rW[fcol // 512]], excl=[rb])

        def a_proj_chunks(cs, n):
            a2 = cx["a2"]
            ra2 = cx["ra2"]
            for c in cs:
                ti = c % 2
                bk, rb = nb()
                zT_group(bk, 0, D + c * 128, n, rb)
                zT_group(bk, 256, c * 128, n, rb)
                S.op("act", lambda e, bk=bk, ti=ti: e.activation(out=th[ti][:, 0:n], in_=bk[:, 0:n], func=AF.Tanh,
                                                                 scale=0.5),
                     excl=[rb], writes=[rth[ti]])
                S.op("dve", lambda e, bk=bk, c=c, ti=ti: e.scalar_tensor_tensor(
                    out=a2[:, c, 30:30 + n], in0=th[ti][:, 0:n], scalar=1.0, in1=bk[:, 256:256 + n],
                    op0=ALU.add, op1=ALU.mult), reads=[rth[ti]], excl=[rb], writes=[ra2[c]])

        def a_gate_chunks(n, par):
            sg, rsg = cx["sg2"][par], cx["rsg2"][par]
            for c2 in range(4):
                bk, rb = nb()
                zT_group(bk, 0, 2 * D + (2 * c2) * 128, n, rb)
                zT_group(bk, 256, 2 * D + (2 * c2 + 1) * 128, n, rb)
                S.op("act", lambda e, bk=bk, c2=c2: e.activation(
                    out=sg[:, 2 * c2:2 * c2 + 2, 0:n],
                    in_=bk[:].rearrange("p (a b) -> p a b", a=2)[:, :, 0:n], func=AF.Silu),
                    excl=[rb], writes=[rsg[c2]])

        def branch_a_proj(n, par):
            a_proj_chunks(range(8), n)
            a_gate_chunks(n, par)

        def a_in_rows_out(col0, nrows, dst):
            a2 = cx["a2"]
            ra2 = cx["ra2"]
            for c in range(8):
                S.op("act", lambda e, c=c: e.activation(out=a2f[:, c, 0:nrows], in_=a2[:, c, col0:col0 + nrows],
                                                        func=AF.Copy),
                     reads=[ra2[c]], writes=[ra2f])
            bks = [nb(), nb()]
            for c in range(8):
                bk, rb = bks[c // 4]
                S.op("pe", lambda e, bk=bk, c=c: e.transpose(out=bk[0:nrows, (c % 4) * 128:(c % 4) * 128 + 128],
                                                             in_=a2f[:, c, 0:nrows], identity=ident[:]),
                     reads=[ra2f, rC["ident"]], excl=[rb])
            for hf in range(2):
                bk, rb = bks[hf]
                S.op("act", lambda e, bk=bk, hf=hf: e.activation(
                    out=vraw[0:nrows, hf * 512:(hf + 1) * 512], in_=bk[0:nrows, :], func=AF.Copy, scale=0.5),
                    excl=[rb], writes=[rvraw])
            dma_out(dst, vraw[0:nrows, :], rvraw, "o_ain")

        def conv_pe(c, n):
            groups = []
            k0 = TD
            while k0 < CW:
                k1 = min(CW, k0 + TS)
                sl = slotctr[0] % NSLOT
                slotctr[0] += 1
                S.op("sp", lambda e, sl=sl, k0=k0, k1=k1: e.dma_start(
                    out=dslot[:, sl, 0:k1 - k0, :],
                    in_=dg[c, :, (k0 - TD) * 128:(k1 - TD) * 128].rearrange("p (k j) -> p k j", j=128)),
                    reads=[rdg[c]], writes=[rslot[sl]], dma="dslot%d" % sl)
                groups.append((sl, k0, k1))
                k0 = k1
            bk, rb = nb()
            S.op("pe", lambda e: e.matmul(
                out=bk[:, 0:n], lhsT=cb8[0:8, :], rhs=identb[0:8, c:c + 1].broadcast_to([8, n]),
                start=True, stop=False), reads=[rC["cb8"], rC["identb"]], excl=[rb])
            for sl, k0, k1 in groups:
                for k in range(k0, k1):
                    S.op("pe", lambda e, k=k, sl=sl, k0=k0: e.matmul(
                        out=bk[:, 0:n], lhsT=dslot[:, sl, k - k0, :], rhs=a2[:, c, k:k + n],
                        start=False, stop=(k == CW - 1)),
                        reads=[rslot[sl], ra2[c]], excl=[rb])
            return bk, rb

        def conv_dve_pair(cs, bks, n):
            for k in range(TD):
                for j, c in enumerate(cs):
                    bk, rb = bks[j]
                    ai = j
                    last = (k == TD - 1)
                    outap = acv[:, c, 0:n] if last else acc[ai][:, 0:n]
                    in1 = bk[:, 0:n] if k == 0 else acc[ai][:, 0:n]
                    S.op("dve", lambda e, c=c, k=k, outap=outap, in1=in1: e.scalar_tensor_tensor(
                        out=outap, in0=a2[:, c, k:k + n], scalar=wh[:, c * CW + k:c * CW + k + 1],
                        in1=in1, op0=ALU.mult, op1=ALU.add),
                        reads=[ra2[c], rC["wh"]] + ([] if k == 0 else [racc[ai]]),
                        excl=([rb] if k == 0 else []),
                        writes=[racv[c]] if last else [racc[ai]])

        def halo_shift(n, cs=range(8)):
            for c in cs:
                S.op("pool", lambda e, c=c: e.tensor_copy(out=a2[:, c, 0:30], in_=a2[:, c, n:n + 30]),
                     reads=[ra2[c]], writes=[ra2[c]])

        def a_layernorm(n, ntile, par, P=128, mid=None):
            sg, rsg = cx["sg2"][par], cx["rsg2"][par]
            acv, racv, mixT, rmix = cx["acv"], cx["racv"], cx["mixT"], cx["rmix"]
            bk, rb = nb()
            first = [True]
            for c in range(8):
                si = c % 2
                S.op("act", lambda e, c=c, si=si: e.activation(out=sq[si][:, 0:n], in_=acv[:, c, 0:n], func=AF.Square),
                     reads=[racv[c]], writes=[rsq[si]])
                for t in range(ntile):
                    st = first[0]
                    first[0] = False
                    S.op("pe", lambda e, c=c, t=t, st=st: e.matmul(
                        out=bk[0:P, 2 * t:2 * t + 2], lhsT=acv[:, c, t * 128:t * 128 + P], rhs=onesb[:, 0:2],
                        start=st, stop=(c == 7), skip_group_check=True),
                        reads=[racv[c], rC["onesb"]], excl=[rb])
                    S.op("pe", lambda e, c=c, t=t, si=si: e.matmul(
                        out=bk[0:P, 8 + 2 * t:8 + 2 * t + 2], lhsT=sq[si][:, t * 128:t * 128 + P], rhs=onesb[:, 0:2],
                        start=False, stop=(c == 7), skip_group_check=True),
                        reads=[rsq[si], rC["onesb"]], excl=[rb])
            S.op("dve", lambda e: e.tensor_scalar(out=small[0:P, AS:AS + ntile], in0=bk[0:P, 0:2 * ntile:2],
                                                  scalar1=1.0 / D, scalar2=None, op0=ALU.mult),
                 excl=[rb], writes=[smr("a0")])
            S.op("dve", lambda e: e.tensor_scalar(out=small[0:P, AS + 2:AS + 2 + ntile], in0=bk[0:P, 8:8 + 2 * ntile:2],
                                                  scalar1=1.0 / D, scalar2=None, op0=ALU.mult),
                 excl=[rb], writes=[smr("a0b")])
            S.op("dve", lambda e: e.tensor_tensor(out=small[0:P, AS + 4:AS + 4 + ntile], in0=small[0:P, AS:AS + ntile],
                                                  in1=small[0:P, AS:AS + ntile], op=ALU.mult),
                 reads=[smr("a0")], writes=[smr("a1")])
            S.op("dve", lambda e: e.tensor_tensor(out=small[0:P, AS + 6:AS + 6 + ntile],
                                                  in0=small[0:P, AS + 2:AS + 2 + ntile],
                                                  in1=small[0:P, AS + 4:AS + 4 + ntile], op=ALU.subtract),
                 reads=[smr("a0b"), smr("a1")], writes=[smr("a2")])
            S.op("dve", lambda e: e.tensor_scalar(out=small[0:P, AS + 6:AS + 6 + ntile],
                                                  in0=small[0:P, AS + 6:AS + 6 + ntile],
                                                  scalar1=EPS, scalar2=None, op0=ALU.add),
                 reads=[smr("a2")], writes=[smr("a2")])
            pool_pow(AS + 6, AS + 8, ntile, P, smr("a2"), smr("a3"))
            S.op("dve", lambda e: e.scalar_tensor_tensor(out=small[0:P, AS + 10:AS + 10 + ntile],
                                                         in0=small[0:P, AS:AS + ntile], scalar=-1.0,
                                                         in1=small[0:P, AS + 8:AS + 8 + ntile],
                                                         op0=ALU.mult, op1=ALU.mult),
                 reads=[smr("a0"), smr("a3")], writes=[smr("a4")])
            for t in range(ntile):
                for q in range(2):
                    col = AS + 8 + 2 * q + t
                    S.op("dve", lambda e, t=t, q=q, col=col: e.tensor_scalar(
                        out=diag[0:P, 2 * q + t, 0:P], in0=ident[0:P, 0:P], scalar1=small[0:P, col:col + 1],
                        scalar2=None, op0=ALU.mult),
                        reads=[rC["ident"], smr("a3"), smr("a4")], writes=[rtb, rtb1])
            if mid is not None:
                mid()
            bk2, rb2 = nb()
            for t in range(ntile):
                for q in range(2):
                    S.op("pe", lambda e, t=t, q=q: e.matmul(
                        out=bk2[:, q * 256 + t * 128:q * 256 + t * 128 + P], lhsT=onesf[0:P, :],
                        rhs=diag[0:P, 2 * q + t, 0:P], start=True, stop=True, skip_group_check=True),
                        reads=[rtb, rtb1, rC["onesf"]], excl=[rb2])
            def fin(c):
                ti = c % 2
                S.op("dve", lambda e: e.tensor_tensor(out=mixT[:, c, 0:n], in0=tmpb[ti][:, 0:n],
                                                      in1=sg[:, c, 0:n], op=ALU.mult),
                     reads=[rtmpb[ti], rsg[c // 2]], writes=[rmix[c]])

            for c in range(8):
                ti = c % 2
                S.op("dve", lambda e, c=c, ti=ti: e.tensor_tensor(out=tmpb[ti][:, 0:n], in0=acv[:, c, 0:n],
                                                                  in1=bk2[:, 0:n], op=ALU.mult),
                     reads=[racv[c]], excl=[rb2], writes=[rtmpb[ti]])
                S.op("dve", lambda e, c=c, ti=ti: e.tensor_tensor(out=tmpb[ti][:, 0:n], in0=tmpb[ti][:, 0:n],
                                                                  in1=bk2[:, 256:256 + n], op=ALU.add),
                     reads=[rtmpb[ti]], excl=[rb2], writes=[rtmpb[ti]])
                S.op("act", lambda e, c=c, ti=ti: e.activation(out=tmpb[ti][:, 0:n], in_=tmpb[ti][:, 0:n], func=AF.Silu,
                                                               scale=alg[:, c:c + 1], bias=alb[:, c:c + 1]),
                     reads=[rtmpb[ti], rC["alg"], rC["alb"]], writes=[rtmpb[ti]])
                if c >= 1:
                    fin(c - 1)
            fin(7)

        def v_tile(t, P=128, raw_dst=None):
            xT = cx["xT"]
            rxT = cx["rxT"]
            bks = [nb(), nb()]
            for half in range(2):
                bk, rb = bks[half]
                for k in range(8):
                    S.op("pe", lambda e, bk=bk, half=half, k=k: e.matmul(
                        out=bk[0:P, :], lhsT=xT[:, k, t * 128:t * 128 + P],
                        rhs=W[:, k, 4 * D + half * 512:4 * D + (half + 1) * 512], start=(k == 0), stop=(k == 7)),
                        reads=[rxT, rW[8 + half]], excl=[rb])
            for half in range(2):
                bk, rb = bks[half]
                S.op("dve", lambda e, bk=bk, half=half: e.bn_stats(
                    out=small[0:P, ST6 + 6 * half:ST6 + 6 * half + 6], in_=bk[0:P, :]),
                    excl=[rb], writes=[smr("st6")])
            S.op("dve", lambda e: e.bn_aggr(out=small[0:P, MV:MV + 2], in_=small[0:P, ST6:ST6 + 12]),
                 reads=[smr("st6")], writes=[smr("mv")])
            S.op("dve", lambda e: e.tensor_scalar(out=small[0:P, MV + 1:MV + 2], in0=small[0:P, MV + 1:MV + 2],
                                                  scalar1=EPS, scalar2=None, op0=ALU.add),
                 reads=[smr("mv")], writes=[smr("mv")])
            pool_pow(MV + 1, VR, 1, P, smr("mv"), smr("vr"))
            S.op("dve", lambda e: e.scalar_tensor_tensor(out=small[0:P, VN:VN + 1], in0=small[0:P, MV:MV + 1],
                                                         scalar=-1.0, in1=small[0:P, VR:VR + 1],
                                                         op0=ALU.mult, op1=ALU.mult),
                 reads=[smr("mv"), smr("vr")], writes=[smr("vn")])
            for half in range(2):
                bk, rb = bks[half]
                S.op("act", lambda e, bk=bk, half=half: e.activation(
                    out=nrm[0:P, t, half * 512:(half + 1) * 512], in_=bk[0:P, :], func=AF.Identity,
                    scale=small[0:P, VR:VR + 1], bias=small[0:P, VN:VN + 1]),
                    reads=[smr("vr"), smr("vn")], excl=[rb], writes=[rnrm[t]])
                if raw_dst is not None:
                    S.op("act", lambda e, bk=bk, half=half: e.activation(
                        out=vraw[0:P, half * 512:(half + 1) * 512], in_=bk[0:P, :], func=AF.Copy),
                        excl=[rb], writes=[rvraw])
            if raw_dst is not None:
                dma_out(raw_dst, vraw[0:P, :], rvraw, "o_v")

        def head_pe(h, n, ntile):
            ui = h % 2
            bk, rb = nb()
            zT_group(bk, 0, 5 * D + h * 128, n, rb)
            zT_group(bk, 256, 3 * D + h * 128, n, rb)
            S.op("act", lambda e: e.activation(out=sgb2[ui][:, 0:n], in_=bk[:, 0:n], func=AF.Silu),
                 excl=[rb], writes=[rsgb2[ui]])
            gk, grb = None, None
            if ntile > 0:
                gk, grb = nb()
                for t in range(ntile):
                    S.op("pe", lambda e, t=t: e.matmul(
                        out=gk[:, t * 128:(t + 1) * 128], lhsT=nrm[:, t, h * 128:(h + 1) * 128], rhs=wsT[:, h, :],
                        start=True, stop=True, skip_group_check=True),
                        reads=[rnrm[t], rC["wsT"]], excl=[grb])
            return (bk, rb, gk, grb)

        def head_dve(h, n, ntile, hb):
            mixT = cx["mixT"]
            rmix = cx["rmix"]
            bk, rb, gk, grb = hb
            ui = h % 2
            S.op("dve", lambda e: e.tensor_tensor(out=ug2[ui][:, 0:n], in0=bk[:, 256:256 + n], in1=sgb2[ui][:, 0:n],
                                                  op=ALU.mult),
                 reads=[rsgb2[ui]], excl=[rb], writes=[rug2[ui]])
            if ntile > 0:
                S.op("dve", lambda e: e.scalar_tensor_tensor(
                    out=tmpa[:, 0:n].rearrange("p (a b) -> p a b", b=128),
                    in0=gk[:, 0:n].rearrange("p (a b) -> p a b", b=128), scalar=blg[:, h:h + 1],
                    in1=Cm[:, h, :].unsqueeze(1).broadcast_to([128, ntile, 128]), op0=ALU.mult, op1=ALU.add),
                    reads=[rC["blg"], rC["Cm"]], excl=[grb], writes=[rtmpa])
                S.op("dve", lambda e: e.tensor_tensor(out=mixT[:, 8 + h, 0:n], in0=tmpa[:, 0:n], in1=ug2[ui][:, 0:n],
                                                      op=ALU.mult),
                     reads=[rtmpa, rug2[ui]], writes=[rmix[8 + h]])

        def conv_and_heads(c0, h0, n, ntile):
            cs = (c0, c0 + 1) if c0 is not None else ()
            hs = (h0, h0 + 1) if h0 is not None else ()
            bks = [conv_pe(c, n) for c in cs]
            hbs = [head_pe(h, n, ntile) for h in hs]
            if cs:
                conv_dve_pair(cs, bks, n)
            for h, hb in zip(hs, hbs):
                head_dve(h, n, ntile, hb)

        def v_tiles(ntile, last_block):
            for t in range(ntile):
                v_tile(t, raw_dst=(cvp_d if (last_block and t == ntile - 1) else None))

        ycnt = [0]

        def out_proj_tiles(tiles):
            mixT = cx["mixT"]
            rmix = cx["rmix"]
            allb = []
            for (t, dst_rows, P, xres_src, xres_row0) in tiles:
                allb.append([nb(), nb()])
            for phase in range(2):
                ks = list(range(8, 16)) if phase == 0 else list(range(8))
                for ti, (t, dst_rows, P, xres_src, xres_row0) in enumerate(tiles):
                    for half in range(2):
                        bk, rb = allb[ti][half]
                        for k in ks:
                            S.op("pe", lambda e, bk=bk, half=half, k=k, t=t, P=P: e.matmul(
                                out=bk[0:P, :], lhsT=mixT[:, k, t * 128:t * 128 + P],
                                rhs=WO[:, k, half * 512:(half + 1) * 512], start=(k == 8), stop=(k == 7)),
                                reads=[rmix[k], rWO[k // 4]], excl=[rb])
            for ti, (t, dst_rows, P, xres_src, xres_row0) in enumerate(tiles):
                yi = ycnt[0] % 2
                ycnt[0] += 1
                ybuf, ry = yb[yi], ryb[yi]
                S.op("pool", lambda e, ybuf=ybuf, P=P, xres_src=xres_src, xres_row0=xres_row0: e.dma_start(
                    out=ybuf[0:P, :], in_=xres_src[xres_row0:xres_row0 + P, :]),
                    writes=[ry], dma="yb%d" % yi)
                for half in range(2):
                    bk, rb = allb[ti][half]
                    S.op("dve", lambda e, bk=bk, half=half, ybuf=ybuf, P=P: e.tensor_tensor(
                        out=ybuf[0:P, half * 512:(half + 1) * 512], in0=bk[0:P, :],
                        in1=ybuf[0:P, half * 512:(half + 1) * 512], op=ALU.add),
                        reads=[ry], excl=[rb], writes=[ry])
                S.op("act", lambda e, ybuf=ybuf, P=P: e.activation(out=xh[0:P, :], in_=ybuf[0:P, :], func=AF.Square,
                                                                   accum_out=small[0:P, YS:YS + 1]),
                     reads=[ry], writes=[smr("ys"), rxh])
                S.op("dve", lambda e, P=P: e.tensor_scalar(out=small[0:P, YS + 1:YS + 2], in0=small[0:P, YS:YS + 1],
                                                           scalar1=1.0 / D, scalar2=EPS, op0=ALU.mult, op1=ALU.add),
                     reads=[smr("ys")], writes=[smr("ym")])
                pool_pow(YS + 1, YS + 2, 1, P, smr("ym"), smr("yr"))
                S.op("dve", lambda e, ybuf=ybuf, P=P: e.scalar_tensor_tensor(
                    out=ybuf[0:P, :], in0=ybuf[0:P, :], scalar=small[0:P, YS + 2:YS + 3], in1=fgB[0:P, :],
                    op0=ALU.mult, op1=ALU.mult),
                    reads=[ry, smr("yr"), rC["fgB"]], writes=[ry])
                r = Res("o")
                S.op("pool", lambda e, dst_rows=dst_rows, ybuf=ybuf, P=P: e.dma_start(out=dst_rows, in_=ybuf[0:P, :]),
                     reads=[ry], writes=[r], dma="yo%d" % yi)
                out_res.append(r)

        NT = NB // 128

        def fe_a(bi, tile):
            if bi < NBLK:
                load_x(bi * NB + tile * 128)
                front_end_a()

        def fe_b(bi, tile):
            if bi < NBLK:
                front_end_b(tile * 128)

        def fe_stats(bi, tile):
            if bi < NBLK:
                load_x(bi * NB + tile * 128)
                par = bi % 2
                front_end_stats(junk=sg2[par][:].rearrange("p a b -> p (a b)"), rjunk=rsg2[par])

        def fe_copy(bi, tile):
            if bi < NBLK:
                front_end_copy()

        n = NS
        st3 = st_d.rearrange("(s k) c -> s k c", k=30)
        css3 = css_d.rearrange("(s k) c -> s k c", k=30)

        def sample_ctx(fn):
            cx.update(cx_sample)
            fn()
            cx.update(cx_prompt)

        def stage_s1():
            rshift = Res("o")
            S.op("sp", lambda e: e.dma_start(out=css3[:, 0:29, :], in_=st3[:, 1:30, :]), writes=[rshift],
                 dma="o_shift")
            out_res.append(rshift)
            for _ in stage_s1_gen():
                pass

        def stage_s1_gen():
            load_x(0, nrows=NS, src=xs_d)
            front_end_a(P=NS)
            yield
            front_end_b(0, P=NS, ntok=NS)
            yield
            a_proj_chunks(range(8), n)
            a_gate_chunks(n, 0)
            yield
            a_in_rows_out(30, n, css3[:, 29, :])

        def stage_s2():
            mixT_, rmix_ = cx["mixT"], cx["rmix"]
            v_tile(0, P=n, raw_dst=cvs_d)
            S.op("act", lambda e: e.activation(out=yb[0][0:n, :], in_=nrm[0:n, 0, :], func=AF.Copy),
                 reads=[rnrm[0]], writes=[ryb[0]])
            bk, rb = nb()
            for c in range(8):
                S.op("pe", lambda e, bk=bk, c=c: e.transpose(out=bk[:, c * 16:c * 16 + n],
                                                             in_=yb[0][0:n, c * 128:(c + 1) * 128],
                                                             identity=ident[0:n, 0:n]),
                     reads=[ryb[0], rC["ident"]], excl=[rb])
            S.op("act", lambda e, bk=bk: e.activation(out=tmpb[0][:, 0:8 * n], in_=bk[:, 0:8 * n], func=AF.Copy),
                 excl=[rb], writes=[rtb])
            for h in range(8):
                hb = head_pe(h, n, 0)
                head_dve(h, n, 0, hb)
                ui = h % 2
                S.op("dve", lambda e, h=h: e.tensor_scalar(out=tmpa[:, 0:n], in0=tmpb[0][:, h * n:(h + 1) * n],
                                                           scalar1=coefA[:, h:h + 1], scalar2=coefB[:, h:h + 1],
                                                           op0=ALU.mult, op1=ALU.add),
                     reads=[rtb, rC["coefA"], rC["coefB"]], writes=[rtmpa])
                S.op("dve", lambda e, h=h, ui=ui: e.tensor_tensor(out=mixT_[:, 8 + h, 0:n], in0=tmpa[:, 0:n],
                                                                  in1=ug2[ui][:, 0:n], op=ALU.mult),
                     reads=[rtmpa, rug2[ui]], writes=[rmix_[8 + h]])

        def stage_s3_gen():
            a2_, ra2_, acv_, racv_ = cx_sample["a2"], cx_sample["ra2"], cx_sample["acv"], cx_sample["racv"]
            dma_in(xh[0:120, :], cw30_d, rxh, "cw30")
            prod = yb[0][:].bitcast(BF16)
            rprod = ryb[0]
            for g in range(4):
                S.op("sp", lambda e, g=g: e.dma_start(out=xf[0:120, :], in_=st_d[g * 120:(g + 1) * 120, :]),
                     writes=[rxf], dma="xf")
                S.op("dve", lambda e: e.tensor_tensor(out=prod[0:120, 0:D], in0=xf[0:120, :], in1=xh[0:120, :],
                                                      op=ALU.mult),
                     reads=[rxf, rxh], writes=[rprod])
                yield
                bkc, rbc = nb()
                for c in range(8):
                    S.op("pe", lambda e, bkc=bkc, c=c: e.matmul(
                        out=bkc[:, c * 4:c * 4 + 4], lhsT=prod[0:120, c * 128:(c + 1) * 128],
                        rhs=sel[0:120, :], start=True, stop=True, skip_group_check=True),
                        reads=[rprod, rC["sel"]], excl=[rbc])
                S.op("dve", lambda e, bkc=bkc, g=g: e.tensor_copy(out=tmpa[:, g * 32:(g + 1) * 32], in_=bkc[:, 0:32]),
                     excl=[rbc], writes=[rtmpa])
            for c in range(8):
                ai = c % 2
                S.op("dve", lambda e, c=c, ai=ai: e.tensor_scalar(
                    out=acc[ai][:, 0:n].rearrange("p (g j) -> p g j", j=4),
                    in0=tmpa[:, 0:128].rearrange("p (g c j) -> p g c j", g=4, j=4)[:, :, c, :],
                    scalar1=cb[:, c:c + 1], scalar2=None,
                    op0=ALU.add), reads=[rC["cb"], rtmpa], writes=[racc[ai]])
                S.op("dve", lambda e, c=c, ai=ai: e.scalar_tensor_tensor(
                    out=acv_[:, c, 0:n], in0=a2_[:, c, 30:30 + n], scalar=wh[:, c * CW + 30:c * CW + 31],
                    in1=acc[ai][:, 0:n], op0=ALU.mult, op1=ALU.add),
                    reads=[ra2_[c], rC["wh"], racc[ai]], writes=[racv_[c]])

        def stage_s3():
            for _ in stage_s3_gen():
                pass

        def stage_s5():
            out_proj_tiles([(0, ys_d[0:n, :], n, xs_d, 0)])

        sample_stages = ["s1", "s3", "s2", "s4", "s5"]

        def emit_sample_stage(prompt_aproj=None, prompt_aproj_parts=None):
            if not sample_stages:
                return False
            st = sample_stages.pop(0)
            used = [False]
            if st == "s1":
                if prompt_aproj_parts is not None:
                    cx.update(cx_sample)
                    rshift = Res("o")
                    S.op("sp", lambda e: e.dma_start(out=css3[:, 0:29, :], in_=st3[:, 1:30, :]), writes=[rshift],
                         dma="o_shift")
                    out_res.append(rshift)
                    gen = stage_s1_gen()
                    for part in prompt_aproj_parts:
                        next(gen, None)
                        cx.update(cx_prompt)
                        part()
                        cx.update(cx_sample)
                    for _ in gen:
                        pass
                    cx.update(cx_prompt)
                    used[0] = True
                else:
                    sample_ctx(stage_s1)
            elif st == "s2":
                sample_ctx(stage_s2)
            elif st == "s3":
                if prompt_aproj_parts is not None:
                    gen = stage_s3_gen()
                    for part in prompt_aproj_parts:
                        next(gen, None)
                        part()
                    for _ in gen:
                        pass
                    used[0] = True
                else:
                    stage_s3()
            elif st == "s4":
                def mid():
                    if prompt_aproj is not None:
                        cx.update(cx_prompt)
                        prompt_aproj()
                        cx.update(cx_sample)
                        used[0] = True
                sample_ctx(lambda: a_layernorm(n, 1, 0, P=n, mid=mid))
            elif st == "s5":
                sample_ctx(stage_s5)
            return used[0]

        def build_diag_scratch():
            for c in range(8):
                q = c % 4
                dst = WO[:, q * 4:(q + 1) * 4, :].rearrange("p a b -> p (a b)")
                nk = CW - TD
                S.op("dve", lambda e, c=c, dst=dst, nk=nk: e.tensor_tensor(
                    out=dst[:, 0:nk * 128].rearrange("p (k j) -> p k j", j=128),
                    in0=identb[:].unsqueeze(1).broadcast_to([128, nk, 128]),
                    in1=wh[:, c * CW + TD:(c + 1) * CW].unsqueeze(2).broadcast_to([128, nk, 128]), op=ALU.mult),
                    reads=[rC["identb"], rC["wh"]], writes=[rWO[q]])
                S.op("sp", lambda e, c=c, dst=dst, nk=nk: e.dma_start(out=dg[c, :, 0:nk * 128], in_=dst[:, 0:nk * 128]),
                     reads=[rWO[q]], writes=[rdg[c]], dma="dgw")

        for tile in range(NT):
            fe_a(0, tile)
            fe_b(0, tile)
        build_diag_scratch()
        a_proj_chunks(range(8), NB)
        if NBLK == 1:
            a_in_rows_out(NB, 30, csp_d)
        a_gate_chunks(NB, 0)
        v_tiles(NT, NBLK == 1)
        load_wout()
        for b in range(NBLK):
            nxt = b + 1 < NBLK
            fe_a(b + 1, 0)
            fe_stats(b + 1, 1)
            conv_and_heads(0, 0, NB, NT)
            conv_and_heads(2, 2, NB, NT)
            conv_and_heads(4, 4, NB, NT)
            conv_and_heads(None, 6, NB, NT)
            fe_b(b + 1, 0)
            fe_copy(b + 1, 1)
            conv_and_heads(6, None, NB, NT)
            fe_b(b + 1, 1)

            def next_aproj(b=b, nxt=nxt):
                if nxt:
                    for pr in range(4):
                        cs = (2 * pr, 2 * pr + 1)
                        halo_shift(NB, cs)
                        a_proj_chunks(cs, NB)
                    if b + 1 == NBLK - 1:
                        a_in_rows_out(NB, 30, csp_d)

            def aproj_part(pr, b=b, nxt=nxt):
                def f():
                    if nxt:
                        cs = (2 * pr, 2 * pr + 1)
                        halo_shift(NB, cs)
                        a_proj_chunks(cs, NB)
                        if pr == 3 and b + 1 == NBLK - 1:
                            a_in_rows_out(NB, 30, csp_d)
                return f

            consumed = False
            if b >= 1:
                consumed = emit_sample_stage(next_aproj, [aproj_part(pr) for pr in range(4)] if nxt else None)
            if not consumed:
                next_aproj()

            def mid(b=b, nxt=nxt):
                if nxt:
                    a_gate_chunks(NB, (b + 1) % 2)
                    v_tiles(NT, b + 1 == NBLK - 1)

            a_layernorm(NB, NT, b % 2, mid=mid)
            out_proj_tiles([(t, y_d[b * NB + t * 128:b * NB + (t + 1) * 128, :], 128, x_d, b * NB + t * 128)
                            for t in range(NT)])

        while sample_stages:
            emit_sample_stage(None)

        S.op("sp", None, reads=out_res)
        S.run()
    return nc


def _host_layout(inp):
    f = np.float32

    def pc(v):
        return np.ascontiguousarray(np.asarray(v, f).reshape(8, 128).T)

    cw = np.asarray(inp["conv_w"], f)[0]
    ws = np.asarray(inp["w_s"], f)[0]
    bs = np.asarray(inp["b_s"], f)[0]
    d = {
        "ng": pc(inp["norm_g"][0]),
        "cw": np.ascontiguousarray(cw.reshape(CW, 8, 128).transpose(2, 1, 0).reshape(128, 8 * CW)),
        "cb": pc(inp["conv_b"][0]),
        "alg": pc(inp["a_ln_g"][0]),
        "alb": pc(inp["a_ln_b"][0]),
        "blg": pc(inp["b_ln_g"][0]),
        "blb": pc(inp["b_ln_b"][0]),
        "ws": np.ascontiguousarray(ws.transpose(1, 0, 2).reshape(128, 1024)),
        "bsB": np.ascontiguousarray(np.broadcast_to(bs.reshape(1, 1024), (128, 1024))),
        "fgB": np.ascontiguousarray(np.broadcast_to(np.asarray(inp["final_g"], f).reshape(1, D), (128, D))),
        "ws00": np.ascontiguousarray(np.broadcast_to(ws[:, 0, 0].reshape(1, 8), (128, 8))),
        "bs0": np.ascontiguousarray(np.broadcast_to(bs[:, 0].reshape(1, 8), (128, 8))),
        "cw30": np.ascontiguousarray(np.tile(cw[0:30], (4, 1))),
        "cb8": np.ascontiguousarray(np.asarray(inp["conv_b"], f)[0].reshape(8, 128)),
        "w_in": np.ascontiguousarray(np.asarray(inp["w_in"], f)[0]),
        "w_out": np.ascontiguousarray(np.asarray(inp["w_out"], f)[0]),
    }
    return d


_PROG = {}


def kernel(**inputs):
    xp = np.asarray(inputs["x_prompt"], np.float32)
    xs = np.asarray(inputs["x_sample"], np.float32)
    st = np.asarray(inputs["state_conv"], np.float32)
    B, SEQ, _ = xp.shape
    NSALL = xs.shape[0]
    NS = NSALL // N_CORES
    shared = _host_layout(inputs)
    key = (SEQ, NS)
    if key not in _PROG:
        _PROG[key] = build_program(SEQ, NS)
    nc = _PROG[key]
    in_maps = []
    for i in range(N_CORES):
        m = dict(shared)
        m["x"] = np.ascontiguousarray(xp[i])
        m["xs"] = np.ascontiguousarray(xs[i * NS:(i + 1) * NS, 0, :])
        m["st"] = np.ascontiguousarray(st[0, i * NS:(i + 1) * NS].reshape(NS * 30, D))
        in_maps.append(m)
    res = run_bass_kernel_spmd(nc, in_maps, core_ids=list(range(N_CORES)))
    r = res.results
    y_prompt = np.stack([r[i]["y"] for i in range(N_CORES)], 0)
    y_sample = np.concatenate([r[i]["ys"] for i in range(N_CORES)], 0).reshape(NSALL, 1, D)
    csp = np.stack([r[i]["csp"] for i in range(N_CORES)], 0)[None]
    css = np.concatenate([r[i]["css"].reshape(NS, 30, D) for i in range(N_CORES)], 0)[None]
    cvp = np.stack([r[i]["cvp"] for i in range(N_CORES)], 0)[None]
    cvs = np.concatenate([r[i]["cvs"] for i in range(N_CORES)], 0).reshape(NSALL, 1, D)[None]
    return (y_prompt.astype(np.float32), y_sample.astype(np.float32), csp.astype(np.float32),
            css.astype(np.float32), cvp.astype(np.float32), cvs.astype(np.float32))
```

```python
import numpy as np
from contextlib import ExitStack
import concourse.bass as bass
import concourse.mybir as mybir
from concourse.bass_utils import run_bass_kernel_spmd

F32 = mybir.dt.float32
BF16 = mybir.dt.bfloat16
AF = mybir.ActivationFunctionType
ALU = mybir.AluOpType

D = 1024
DIN = 6144
CW = 31
EPS = 1e-6
NB = 256
TD = 7
N_CORES = 8


class Res:
    __slots__ = ("name", "last_writer", "readers")

    def __init__(self, name=""):
        self.name = name
        self.last_writer = None
        self.readers = []


class Op:
    __slots__ = ("eng", "fn", "idx", "waits", "signal", "semval", "dma_sem", "dma_val", "vc", "dwait")


class Sched:
    ENGS = ("pe", "act", "dve", "pool", "sp")

    def __init__(self, nc):
        self.nc = nc
        self.ops = {e: [] for e in self.ENGS}
        self.all_ops = []
        self.dma_cnt = {}
        self.know = {e: {} for e in self.ENGS}

    def op(self, eng, fn, reads=(), writes=(), excl=(), dma=None):
        o = Op()
        o.eng = eng
        o.fn = fn
        o.idx = len(self.ops[eng]) + 1
        o.signal = False
        o.semval = None
        o.dma_sem = dma
        o.dma_val = None
        o.dwait = {}
        deps = []
        for r in reads:
            if r.last_writer is not None:
                deps.append(r.last_writer)
        for w in writes:
            if w.last_writer is not None:
                deps.append(w.last_writer)
            deps.extend(w.readers)
        for x in excl:
            if x.last_writer is not None and x.last_writer.eng != eng:
                deps.append(x.last_writer)
        for x in excl:
            x.last_writer = o
            x.readers = []
        for r in reads:
            r.readers.append(o)
        for w in writes:
            w.last_writer = o
            w.readers = []
        K = self.know[eng]
        waits = []

        def clk(d):
            return ("dma", d.dma_sem) if d.dma_sem is not None else ("eng", d.eng)

        def cval(d):
            return self.dma_cnt[d.dma_sem] if d.dma_sem is not None else d.idx

        seen = set()
        for d in sorted(deps, key=lambda d: -cval(d)):
            if id(d) in seen or d is o:
                continue
            seen.add(id(d))
            c = clk(d)
            if c == ("eng", "pe") and eng == "pe":
                continue
            if K.get(c, 0) >= cval(d):
                continue
            waits.append(d)
            if d.dma_sem is not None:
                o.dwait[id(d)] = cval(d)
            for k, v in d.vc.items():
                if K.get(k, 0) < v:
                    K[k] = v
            K[c] = cval(d)
        o.waits = waits
        o.vc = dict(K)
        if dma is not None:
            self.dma_cnt[dma] = self.dma_cnt.get(dma, 0) + 1
            o.dma_val = self.dma_cnt[dma]
        self.ops[eng].append(o)
        self.all_ops.append(o)
        return o

    def run(self):
        nc = self.nc
        for o in self.all_ops:
            for d in o.waits:
                if d.dma_sem is None:
                    d.signal = True
        for e in self.ENGS:
            c = 0
            for o in self.ops[e]:
                if o.signal:
                    c += 1
                    o.semval = c
        with ExitStack() as es:
            esem = {e: es.enter_context(nc.semaphore("s_" + e)) for e in self.ENGS}
            dsem = {k: es.enter_context(nc.semaphore("d_" + str(k))) for k in self.dma_cnt}
            block = es.enter_context(nc.Block())

            def emit(e, eng):
                for o in self.ops[e]:
                    for d in o.waits:
                        if d.dma_sem is not None:
                            eng.wait_ge(dsem[d.dma_sem], 16 * o.dwait[id(d)])
                        else:
                            eng.wait_ge(esem[d.eng], d.semval)
                    if o.fn is None:
                        continue
                    ins = o.fn(eng)
                    if o.dma_sem is not None:
                        ins.then_inc(dsem[o.dma_sem], 16)
                    elif o.signal:
                        ins.then_inc(esem[e], 1)

            @block.tensor
            def _(eng):
                emit("pe", eng)

            @block.scalar
            def _(eng):
                emit("act", eng)

            @block.vector
            def _(eng):
                emit("dve", eng)

            @block.gpsimd
            def _(eng):
                emit("pool", eng)

            @block.sync
            def _(eng):
                emit("sp", eng)


def build_program(SEQ=2048, NS=16):
    assert SEQ % NB == 0 and NS == 16
    NBLK = SEQ // NB
    nc = bass.Bass("TRN2", target_bir_lowering=False)

    def din(name, shape):
        return nc.dram_tensor(name, list(shape), F32, kind="ExternalInput").ap()

    def dout(name, shape):
        return nc.dram_tensor(name, list(shape), F32, kind="ExternalOutput").ap()

    x_d = din("x", [SEQ, D])
    xs_d = din("xs", [NS, D])
    st_d = din("st", [NS * 30, D])
    win_d = din("w_in", [D, DIN])
    wout_d = din("w_out", [2 * D, D])
    ng_d = din("ng", [128, 8])
    cw_d = din("cw", [128, 8 * CW])
    cb_d = din("cb", [128, 8])
    alg_d = din("alg", [128, 8])
    alb_d = din("alb", [128, 8])
    blg_d = din("blg", [128, 8])
    blb_d = din("blb", [128, 8])
    ws_d = din("ws", [128, 8 * 128])
    bsB_d = din("bsB", [128, 8 * 128])
    fgB_d = din("fgB", [128, D])
    ws00_d = din("ws00", [128, 8])
    bs0_d = din("bs0", [128, 8])
    cw30_d = din("cw30", [120, D])
    cb8_d = din("cb8", [8, 128])

    y_d = dout("y", [SEQ, D])
    ys_d = dout("ys", [NS, D])
    csp_d = dout("csp", [30, D])
    css_d = dout("css", [NS * 30, D])
    cvp_d = dout("cvp", [128, D])
    cvs_d = dout("cvs", [NS, D])

    S = Sched(nc)
    es = ExitStack()
    with es:
        def sb(name, shape, dt=F32):
            return es.enter_context(nc.sbuf_tensor("s_" + name, list(shape), dt))

        W = sb("W", [128, 8, DIN], BF16)
        WO = sb("WO", [128, 16, D], BF16)
        rW = [Res("W%d" % i) for i in range(12)]
        rWO = [Res("WO%d" % i) for i in range(4)]
        ident = sb("ident", [128, 128])
        onesf = sb("onesf", [128, 128])
        onesb = sb("onesb", [128, 128], BF16)
        ng = sb("ng", [128, 8])
        wh = sb("wh", [128, 8 * CW])
        cb = sb("cb", [128, 8])
        alg = sb("alg", [128, 8])
        alb = sb("alb", [128, 8])
        blg = sb("blg", [128, 8])
        blb = sb("blb", [128, 8])
        wsT = sb("wsT", [128, 8, 128], BF16)
        Cm = sb("Cm", [128, 8, 128])
        fgB = sb("fgB", [128, D])
        coefA = sb("coefA", [128, 8])
        coefB = sb("coefB", [128, 8])
        ws00 = sb("ws00", [128, 8])
        bs0 = sb("bs0", [128, 8])
        mhalf = sb("mhalf", [128, 4])
        sel = sb("sel", [128, 4], BF16)
        self32 = sb("self32", [128, 4])
        identb = sb("identb", [128, 128], BF16)
        cb8 = sb("cb8", [8, 128], BF16)
        TPE = CW - TD
        TS = 8
        NSLOT = 4
        dslot = sb("dslot", [128, NSLOT, TS, 128], BF16)
        rslot = [Res("dslot%d" % i) for i in range(NSLOT)]
        slotctr = [0]
        dg = nc.dram_tensor("dg_scratch", [8, 128, (CW - TD) * 128], BF16, kind="ExternalOutput").ap()
        rdg = [Res("dg%d" % c) for c in range(8)]
        rC = {k: Res(k) for k in ["ident", "onesf", "onesb", "ng", "wh", "cb", "alg", "alb", "blg", "blb",
                                  "wsT", "Cm", "fgB", "coefA", "coefB", "ws00", "bs0", "mhalf", "sel",
                                  "self32", "identb", "cb8"]}
        xf = sb("xf", [128, D])
        rxf = Res("xf")
        xh = sb("xh", [128, D])
        rxh = Res("xh")
        vraw = xh
        rvraw = rxh
        xT = sb("xT", [128, 8, NB], BF16)
        rxT = Res("xT")
        a2 = sb("a2", [128, 8, 30 + NB], BF16)
        ra2 = [Res("a2_%d" % c) for c in range(8)]
        th0 = sb("th0", [128, NB])
        th = [th0, th0]
        rth0 = Res("th0")
        rth = [rth0, rth0]
        acc = [sb("acc%d" % i, [128, NB]) for i in range(2)]
        racc = [Res("acc%d" % i) for i in range(2)]
        acv = sb("acv", [128, 8, NB], BF16)
        racv = [Res("acv%d" % c) for c in range(8)]
        sq = [sb("sq%d" % i, [128, NB], BF16) for i in range(2)]
        rsq = [Res("sq%d" % i) for i in range(2)]
        sg2 = [sb("sg_%d" % i, [128, 8, NB], BF16) for i in range(2)]
        rsg2 = [[Res("sg%d_%d" % (i, c)) for c in range(4)] for i in range(2)]
        sgb2 = [sb("sgb%d" % i, [128, NB]) for i in range(2)]
        rsgb2 = [Res("sgb%d" % i) for i in range(2)]
        ug2 = [sb("ug%d" % i, [128, NB]) for i in range(2)]
        rug2 = [Res("ug%d" % i) for i in range(2)]
        tmpb2 = sb("tmpb", [128, 2, NB])
        tmpb = [tmpb2[:, 0, :], tmpb2[:, 1, :]]
        rtb = Res("tmpb0")
        rtb1 = Res("tmpb1")
        rtmpb = [rtb, rtb1]
        diag = tmpb2[:].rearrange("p a (b c) -> p (a b) c", c=128)
        rdiag = rtb
        nrm = sb("nrm", [128, 2, D], BF16)
        rnrm = [Res("nrm%d" % i) for i in range(2)]
        mixT = sb("mixT", [128, 16, NB], BF16)
        rmix = [Res("mix%d" % c) for c in range(16)]
        tmpa = sb("tmpa", [128, NB])
        rtmpa = Res("tmpa")
        a2f = tmpa[:].rearrange("p (c j) -> p c j", c=8)
        ra2f = rtmpa
        yb0 = sb("yb0", [128, D])
        yb = [yb0, yb0]
        ryb0 = Res("yb0")
        ryb = [ryb0, ryb0]
        small = sb("small", [128, 64])
        rsm = {}
        xT_s = sb("xT_s", [128, 8, NS], BF16)
        a2_s = sb("a2_s", [128, 8, 30 + NS], BF16)
        acv_s = sb("acv_s", [128, 8, NS], BF16)
        sg_s = sb("sg_s", [128, 8, NS], BF16)
        mixT_s = sb("mixT_s", [128, 16, NS], BF16)
        cx_prompt = dict(xT=xT, rxT=rxT, a2=a2, ra2=ra2, acv=acv, racv=racv, sg2=sg2, rsg2=rsg2, mixT=mixT, rmix=rmix)
        rsg_s = [Res("sgs%d" % c) for c in range(4)]
        cx_sample = dict(xT=xT_s, rxT=Res("xT_s"), a2=a2_s, ra2=[Res("a2s%d" % c) for c in range(8)],
                         acv=acv_s, racv=[Res("acvs%d" % c) for c in range(8)],
                         sg2=[sg_s, sg_s], rsg2=[rsg_s, rsg_s], mixT=mixT_s, rmix=[Res("mixs%d" % c) for c in range(16)])
        cx = dict(cx_prompt)

        def smr(k):
            if k not in rsm:
                rsm[k] = Res("sm" + k)
            return rsm[k]

        SS, MS, RSTD = 0, 2, 4
        ST6 = 8
        MV = 20
        VR, VN = 22, 23
        AS = 24
        YS = 40

        banks = [es.enter_context(nc.psum_tensor("pb%d" % i, [128, 512], F32)) for i in range(8)]
        rbank = [Res("pb%d" % i) for i in range(8)]
        bctr = [0]

        def nb():
            i = bctr[0] % 8
            bctr[0] += 1
            return banks[i], rbank[i]

        out_res = []

        def dma_out(dst, src, rsrc, key):
            r = Res("o")
            S.op("sp", lambda e: e.dma_start(out=dst, in_=src), reads=[rsrc], writes=[r], dma=key)
            out_res.append(r)

        def dma_in(dst, src, res, key, eng="sp"):
            S.op(eng, lambda e: e.dma_start(out=dst, in_=src), writes=[res], dma=key)

        def load_x(row0, nrows=128, src=None):
            src = x_d if src is None else src
            S.op("sp", lambda e: e.dma_start(out=xf[0:nrows, :], in_=src[row0:row0 + nrows, :]),
                 writes=[rxf], dma="xf")

        load_x(0)
        dma_in(ng[:], ng_d, rC["ng"], "c0")
        dma_in(wh[:], cw_d, rC["wh"], "c0")
        dma_in(cb[:], cb_d, rC["cb"], "c0")
        dma_in(alg[:], alg_d, rC["alg"], "c0")
        dma_in(alb[:], alb_d, rC["alb"], "c0")
        dma_in(blg[:], blg_d, rC["blg"], "c0")
        dma_in(blb[:], blb_d, rC["blb"], "c0")
        dma_in(ws00[:], ws00_d, rC["ws00"], "c0")
        dma_in(bs0[:], bs0_d, rC["bs0"], "c0")
        dma_in(fgB[:], fgB_d, rC["fgB"], "c1")
        dma_in(yb[0][:], ws_d, ryb[0], "c1")
        dma_in(Cm[:].rearrange("p h t -> p (h t)"), bsB_d, rC["Cm"], "c1")

        w_order = [2, 3, 0, 1, 4, 5, 8, 9, 10, 11, 6, 7]
        for cbk in w_order:
            S.op("pool", lambda e, cbk=cbk: e.dma_start(
                out=W[:, :, cbk * 512:(cbk + 1) * 512],
                in_=win_d[:, cbk * 512:(cbk + 1) * 512].rearrange("(kc p) n -> p kc n", p=128)),
                writes=[rW[cbk]], dma="W%d" % cbk)
            if cbk == 2:
                S.op("pool", lambda e: e.memset(ident[:], 0.0), writes=[rC["ident"]])
                S.op("pool", lambda e: e.affine_select(out=ident[:], in_=ident[:], pattern=[[-1, 128]],
                                                       compare_op=ALU.not_equal, fill=1.0, base=0,
                                                       channel_multiplier=1),
                     reads=[rC["ident"]], writes=[rC["ident"]])
                S.op("pool", lambda e: e.tensor_copy(out=identb[:], in_=ident[:]), reads=[rC["ident"]],
                     writes=[rC["identb"]])
                S.op("pool", lambda e: e.memset(mhalf[:], -0.5), writes=[rC["mhalf"]])
                S.op("pool", lambda e: e.memset(onesf[:], 1.0), writes=[rC["onesf"]])
                S.op("pool", lambda e: e.memset(onesb[:], 1.0), writes=[rC["onesb"]])
                for c in range(8):
                    S.op("pool", lambda e, c=c: e.memset(a2[:, c, 0:30], 0.0), writes=[ra2[c]])
        S.op("pool", lambda e: e.dma_start(out=cb8[:], in_=cb8_d), writes=[rC["cb8"]], dma="cb8")
        def load_wout():
            for q in range(4):
                S.op("pool", lambda e, q=q: e.dma_start(
                    out=WO[:, q * 4:(q + 1) * 4, :],
                    in_=wout_d[q * 512:(q + 1) * 512, :].rearrange("(kc p) n -> p kc n", p=128)),
                    writes=[rWO[q]], dma="WO%d" % q)

        S.op("dve", lambda e: e.tensor_scalar(out=wh[:], in0=wh[:], scalar1=0.5, scalar2=None, op0=ALU.mult),
             reads=[rC["wh"]], writes=[rC["wh"]])
        wsn = yb[0][:].rearrange("p (h s) -> p h s", h=8)
        S.op("pool", lambda e: e.affine_select(out=wsn, in_=wsn, pattern=[[0, 8], [-1, 128]],
                                               compare_op=ALU.is_ge, fill=0.0, base=0, channel_multiplier=1),
             reads=[ryb[0]], writes=[ryb[0]])
        for half in range(2):
            bk, rb = nb()
            for j in range(4):
                h = half * 4 + j
                S.op("pe", lambda e, bk=bk, j=j, h=h: e.transpose(out=bk[:, j * 128:(j + 1) * 128],
                                                                 in_=yb[0][:, h * 128:(h + 1) * 128],
                                                                 identity=ident[:]),
                     reads=[ryb[0], rC["ident"]], excl=[rb])
            S.op("act", lambda e, bk=bk, half=half: e.activation(
                out=wsT[:, half * 4:(half + 1) * 4, :].rearrange("p h t -> p (h t)"), in_=bk[:], func=AF.Copy),
                excl=[rb], writes=[rC["wsT"]])
        for half in range(2):
            bk, rb = nb()
            S.op("pe", lambda e, bk=bk, half=half: e.matmul(
                out=bk[:], lhsT=onesb[:, 0:128], rhs=wsT[:, half * 4:(half + 1) * 4, :].rearrange("p h t -> p (h t)"),
                start=True, stop=True), reads=[rC["onesb"], rC["wsT"]], excl=[rb])
            for j in range(4):
                h = half * 4 + j
                S.op("dve", lambda e, bk=bk, j=j, h=h: e.scalar_tensor_tensor(
                    out=Cm[:, h, :], in0=bk[:, j * 128:(j + 1) * 128], scalar=blb[:, h:h + 1],
                    in1=Cm[:, h, :], op0=ALU.mult, op1=ALU.add),
                    reads=[rC["blb"], rC["Cm"]], excl=[rb], writes=[rC["Cm"]])
        S.op("dve", lambda e: e.tensor_tensor(out=coefA[:], in0=ws00[:], in1=blg[:], op=ALU.mult),
             reads=[rC["ws00"], rC["blg"]], writes=[rC["coefA"]])
        S.op("dve", lambda e: e.tensor_tensor(out=coefB[:], in0=ws00[:], in1=blb[:], op=ALU.mult),
             reads=[rC["ws00"], rC["blb"]], writes=[rC["coefB"]])
        S.op("dve", lambda e: e.tensor_tensor(out=coefB[:], in0=coefB[:], in1=bs0[:], op=ALU.add),
             reads=[rC["coefB"], rC["bs0"]], writes=[rC["coefB"]])
        S.op("pool", lambda e: e.memset(self32[:], 1.0), writes=[rC["self32"]])
        S.op("pool", lambda e: e.affine_select(out=self32[:], in_=self32[:], pattern=[[-30, 4]],
                                               compare_op=ALU.is_ge, fill=0.0, base=0, channel_multiplier=1),
             reads=[rC["self32"]], writes=[rC["self32"]])
        S.op("pool", lambda e: e.affine_select(out=self32[:], in_=self32[:], pattern=[[30, 4]],
                                               compare_op=ALU.is_gt, fill=0.0, base=30, channel_multiplier=-1),
             reads=[rC["self32"]], writes=[rC["self32"]])
        S.op("pool", lambda e: e.tensor_copy(out=sel[:], in_=self32[:]), reads=[rC["self32"]], writes=[rC["sel"]])

        def pool_pow(col_in, col_out, n, P, rin, rout):
            S.op("pool", lambda e: e.tensor_tensor(out=small[0:P, col_out:col_out + n],
                                                   in0=small[0:P, col_in:col_in + n],
                                                   in1=mhalf[0:P, 0:n], op=ALU.pow),
                 reads=[rin, rC["mhalf"]], writes=[rout])

        def front_end_stats(P=128, junk=None, rjunk=None):
            if junk is None:
                junk, rjunk = xh, [rxh]
            S.op("act", lambda e: e.activation(out=junk[0:P, 0:D], in_=xf[0:P, :], func=AF.Square,
                                               accum_out=small[0:P, SS:SS + 1]),
                 reads=[rxf], writes=[smr("ss")] + list(rjunk))
            S.op("dve", lambda e: e.tensor_scalar(out=small[0:P, MS:MS + 1], in0=small[0:P, SS:SS + 1],
                                                  scalar1=1.0 / D, scalar2=EPS, op0=ALU.mult, op1=ALU.add),
                 reads=[smr("ss")], writes=[smr("ms")])
            pool_pow(MS, RSTD, 1, P, smr("ms"), smr("rstd"))

        def front_end_copy(P=128):
            S.op("act", lambda e: e.activation(out=xh[0:P, :], in_=xf[0:P, :], func=AF.Copy,
                                               scale=small[0:P, RSTD:RSTD + 1]),
                 reads=[rxf, smr("rstd")], writes=[rxh])

        def front_end_a(P=128):
            front_end_stats(P)
            front_end_copy(P)

        def front_end_b(tcol, P=128, ntok=128):
            xT = cx["xT"]
            rxT = cx["rxT"]
            for half in range(2):
                bk, rb = nb()
                for j in range(4):
                    k = half * 4 + j
                    S.op("pe", lambda e, bk=bk, j=j, k=k: e.transpose(
                        out=bk[:, j * 128:j * 128 + P], in_=xh[0:P, k * 128:(k + 1) * 128],
                        identity=ident[0:P, 0:P]),
                        reads=[rxh, rC["ident"]], excl=[rb])
                for j in range(4):
                    k = half * 4 + j
                    S.op("act", lambda e, bk=bk, j=j, k=k: e.activation(
                        out=xT[:, k, tcol:tcol + ntok], in_=bk[:, j * 128:j * 128 + ntok], func=AF.Copy,
                        scale=ng[:, k:k + 1]),
                        reads=[rC["ng"]], excl=[rb], writes=[rxT])

        def front_end(tcol, P=128, ntok=128):
            front_end_a(P)
            front_end_b(tcol, P, ntok)

        def zT_group(bk, col0, fcol, n, rb):
            xT = cx["xT"]
            rxT = cx["rxT"]
            for k in range(8):
                S.op("pe", lambda e, k=k: e.matmul(out=bk[:, col0:col0 + n], lhsT=W[:, k, fcol:fcol + 128],
                                                   rhs=xT[:, k, 0:n], start=(k == 0), stop=(k == 7)),
                     reads=[rxT, rW[fcol // 512]], excl=[rb])

        def a_proj_chunks(cs, n):
            a2 = cx["a2"]
            ra2 = cx["ra2"]
            for c in cs:
                ti = c % 2
                bk, rb = nb()
                zT_group(bk, 0, D + c * 128, n, rb)
                zT_group(bk, 256, c * 128, n, rb)
                S.op("act", lambda e, bk=bk, ti=ti: e.activation(out=th[ti][:, 0:n], in_=bk[:, 0:n], func=AF.Tanh,
                                                                 scale=0.5),
                     excl=[rb], writes=[rth[ti]])
                S.op("dve", lambda e, bk=bk, c=c, ti=ti: e.scalar_tensor_tensor(
                    out=a2[:, c, 30:30 + n], in0=th[ti][:, 0:n], scalar=1.0, in1=bk[:, 256:256 + n],
                    op0=ALU.add, op1=ALU.mult), reads=[rth[ti]], excl=[rb], writes=[ra2[c]])

        def a_gate_chunks(n, par):
            sg, rsg = cx["sg2"][par], cx["rsg2"][par]
            for c2 in range(4):
                bk, rb = nb()
                zT_group(bk, 0, 2 * D + (2 * c2) * 128, n, rb)
                zT_group(bk, 256, 2 * D + (2 * c2 + 1) * 128, n, rb)
                S.op("act", lambda e, bk=bk, c2=c2: e.activation(
                    out=sg[:, 2 * c2:2 * c2 + 2, 0:n],
                    in_=bk[:].rearrange("p (a b) -> p a b", a=2)[:, :, 0:n], func=AF.Silu),
                    excl=[rb], writes=[rsg[c2]])

        def branch_a_proj(n, par):
            a_proj_chunks(range(8), n)
            a_gate_chunks(n, par)

        def a_in_rows_out(col0, nrows, dst):
            a2 = cx["a2"]
            ra2 = cx["ra2"]
            for c in range(8):
                S.op("act", lambda e, c=c: e.activation(out=a2f[:, c, 0:nrows], in_=a2[:, c, col0:col0 + nrows],
                                                        func=AF.Copy),
                     reads=[ra2[c]], writes=[ra2f])
            bks = [nb(), nb()]
            for c in range(8):
                bk, rb = bks[c // 4]
                S.op("pe", lambda e, bk=bk, c=c: e.transpose(out=bk[0:nrows, (c % 4) * 128:(c % 4) * 128 + 128],
                                                             in_=a2f[:, c, 0:nrows], identity=ident[:]),
                     reads=[ra2f, rC["ident"]], excl=[rb])
            for hf in range(2):
                bk, rb = bks[hf]
                S.op("act", lambda e, bk=bk, hf=hf: e.activation(
                    out=vraw[0:nrows, hf * 512:(hf + 1) * 512], in_=bk[0:nrows, :], func=AF.Copy, scale=0.5),
                    excl=[rb], writes=[rvraw])
            dma_out(dst, vraw[0:nrows, :], rvraw, "o_ain")

        def conv_pe(c, n):
            groups = []
            k0 = TD
            while k0 < CW:
                k1 = min(CW, k0 + TS)
                sl = slotctr[0] % NSLOT
                slotctr[0] += 1
                S.op("sp", lambda e, sl=sl, k0=k0, k1=k1: e.dma_start(
                    out=dslot[:, sl, 0:k1 - k0, :],
                    in_=dg[c, :, (k0 - TD) * 128:(k1 - TD) * 128].rearrange("p (k j) -> p k j", j=128)),
                    reads=[rdg[c]], writes=[rslot[sl]], dma="dslot%d" % sl)
                groups.append((sl, k0, k1))
                k0 = k1
            bk, rb = nb()
            S.op("pe", lambda e: e.matmul(
                out=bk[:, 0:n], lhsT=cb8[0:8, :], rhs=identb[0:8, c:c + 1].broadcast_to([8, n]),
                start=True, stop=False), reads=[rC["cb8"], rC["identb"]], excl=[rb])
            for sl, k0, k1 in groups:
                for k in range(k0, k1):
                    S.op("pe", lambda e, k=k, sl=sl, k0=k0: e.matmul(
                        out=bk[:, 0:n], lhsT=dslot[:, sl, k - k0, :], rhs=a2[:, c, k:k + n],
                        start=False, stop=(k == CW - 1)),
                        reads=[rslot[sl], ra2[c]], excl=[rb])
            return bk, rb

        def conv_dve_pair(cs, bks, n):
            for k in range(TD):
                for j, c in enumerate(cs):
                    bk, rb = bks[j]
                    ai = j
                    last = (k == TD - 1)
                    outap = acv[:, c, 0:n] if last else acc[ai][:, 0:n]
                    in1 = bk[:, 0:n] if k == 0 else acc[ai][:, 0:n]
                    S.op("dve", lambda e, c=c, k=k, outap=outap, in1=in1: e.scalar_tensor_tensor(
                        out=outap, in0=a2[:, c, k:k + n], scalar=wh[:, c * CW + k:c * CW + k + 1],
                        in1=in1, op0=ALU.mult, op1=ALU.add),
                        reads=[ra2[c], rC["wh"]] + ([] if k == 0 else [racc[ai]]),
                        excl=([rb] if k == 0 else []),
                        writes=[racv[c]] if last else [racc[ai]])

        def halo_shift(n, cs=range(8)):
            for c in cs:
                S.op("pool", lambda e, c=c: e.tensor_copy(out=a2[:, c, 0:30], in_=a2[:, c, n:n + 30]),
                     reads=[ra2[c]], writes=[ra2[c]])

        def a_layernorm(n, ntile, par, P=128, mid=None):
            sg, rsg = cx["sg2"][par], cx["rsg2"][par]
            acv, racv, mixT, rmix = cx["acv"], cx["racv"], cx["mixT"], cx["rmix"]
            bk, rb = nb()
            first = [True]
            for c in range(8):
                si = c % 2
                S.op("act", lambda e, c=c, si=si: e.activation(out=sq[si][:, 0:n], in_=acv[:, c, 0:n], func=AF.Square),
                     reads=[racv[c]], writes=[rsq[si]])
                for t in range(ntile):
                    st = first[0]
                    first[0] = False
                    S.op("pe", lambda e, c=c, t=t, st=st: e.matmul(
                        out=bk[0:P, 2 * t:2 * t + 2], lhsT=acv[:, c, t * 128:t * 128 + P], rhs=onesb[:, 0:2],
                        start=st, stop=(c == 7), skip_group_check=True),
                        reads=[racv[c], rC["onesb"]], excl=[rb])
                    S.op("pe", lambda e, c=c, t=t, si=si: e.matmul(
                        out=bk[0:P, 8 + 2 * t:8 + 2 * t + 2], lhsT=sq[si][:, t * 128:t * 128 + P], rhs=onesb[:, 0:2],
                        start=False, stop=(c == 7), skip_group_check=True),
                        reads=[rsq[si], rC["onesb"]], excl=[rb])
            S.op("dve", lambda e: e.tensor_scalar(out=small[0:P, AS:AS + ntile], in0=bk[0:P, 0:2 * ntile:2],
                                                  scalar1=1.0 / D, scalar2=None, op0=ALU.mult),
                 excl=[rb], writes=[smr("a0")])
            S.op("dve", lambda e: e.tensor_scalar(out=small[0:P, AS + 2:AS + 2 + ntile], in0=bk[0:P, 8:8 + 2 * ntile:2],
                                                  scalar1=1.0 / D, scalar2=None, op0=ALU.mult),
                 excl=[rb], writes=[smr("a0b")])
            S.op("dve", lambda e: e.tensor_tensor(out=small[0:P, AS + 4:AS + 4 + ntile], in0=small[0:P, AS:AS + ntile],
                                                  in1=small[0:P, AS:AS + ntile], op=ALU.mult),
                 reads=[smr("a0")], writes=[smr("a1")])
            S.op("dve", lambda e: e.tensor_tensor(out=small[0:P, AS + 6:AS + 6 + ntile],
                                                  in0=small[0:P, AS + 2:AS + 2 + ntile],
                                                  in1=small[0:P, AS + 4:AS + 4 + ntile], op=ALU.subtract),
                 reads=[smr("a0b"), smr("a1")], writes=[smr("a2")])
            S.op("dve", lambda e: e.tensor_scalar(out=small[0:P, AS + 6:AS + 6 + ntile],
                                                  in0=small[0:P, AS + 6:AS + 6 + ntile],
                                                  scalar1=EPS, scalar2=None, op0=ALU.add),
                 reads=[smr("a2")], writes=[smr("a2")])
            pool_pow(AS + 6, AS + 8, ntile, P, smr("a2"), smr("a3"))
            S.op("dve", lambda e: e.scalar_tensor_tensor(out=small[0:P, AS + 10:AS + 10 + ntile],
                                                         in0=small[0:P, AS:AS + ntile], scalar=-1.0,
                                                         in1=small[0:P, AS + 8:AS + 8 + ntile],
                                                         op0=ALU.mult, op1=ALU.mult),
                 reads=[smr("a0"), smr("a3")], writes=[smr("a4")])
            for t in range(ntile):
                for q in range(2):
                    col = AS + 8 + 2 * q + t
                    S.op("dve", lambda e, t=t, q=q, col=col: e.tensor_scalar(
                        out=diag[0:P, 2 * q + t, 0:P], in0=ident[0:P, 0:P], scalar1=small[0:P, col:col + 1],
                        scalar2=None, op0=ALU.mult),
                        reads=[rC["ident"], smr("a3"), smr("a4")], writes=[rtb, rtb1])
            if mid is not None:
                mid()
            bk2, rb2 = nb()
            for t in range(ntile):
                for q in range(2):
                    S.op("pe", lambda e, t=t, q=q: e.matmul(
                        out=bk2[:, q * 256 + t * 128:q * 256 + t * 128 + P], lhsT=onesf[0:P, :],
                        rhs=diag[0:P, 2 * q + t, 0:P], start=True, stop=True, skip_group_check=True),
                        reads=[rtb, rtb1, rC["onesf"]], excl=[rb2])
            def fin(c):
                ti = c % 2
                S.op("dve", lambda e: e.tensor_tensor(out=mixT[:, c, 0:n], in0=tmpb[ti][:, 0:n],
                                                      in1=sg[:, c, 0:n], op=ALU.mult),
                     reads=[rtmpb[ti], rsg[c // 2]], writes=[rmix[c]])

            for c in range(8):
                ti = c % 2
                S.op("dve", lambda e, c=c, ti=ti: e.tensor_tensor(out=tmpb[ti][:, 0:n], in0=acv[:, c, 0:n],
                                                                  in1=bk2[:, 0:n], op=ALU.mult),
                     reads=[racv[c]], excl=[rb2], writes=[rtmpb[ti]])
                S.op("dve", lambda e, c=c, ti=ti: e.tensor_tensor(out=tmpb[ti][:, 0:n], in0=tmpb[ti][:, 0:n],
                                                                  in1=bk2[:, 256:256 + n], op=ALU.add),
                     reads=[rtmpb[ti]], excl=[rb2], writes=[rtmpb[ti]])
                S.op("act", lambda e, c=c, ti=ti: e.activation(out=tmpb[ti][:, 0:n], in_=tmpb[ti][:, 0:n], func=AF.Silu,
                                                               scale=alg[:, c:c + 1], bias=alb[:, c:c + 1]),
                     reads=[rtmpb[ti], rC["alg"], rC["alb"]], writes=[rtmpb[ti]])
                if c >= 1:
                    fin(c - 1)
            fin(7)

        def v_tile(t, P=128, raw_dst=None):
            xT = cx["xT"]
            rxT = cx["rxT"]
            bks = [nb(), nb()]
            for half in range(2):
                bk, rb = bks[half]
                for k in range(8):
                    S.op("pe", lambda e, bk=bk, half=half, k=k: e.matmul(
                        out=bk[0:P, :], lhsT=xT[:, k, t * 128:t * 128 + P],
                        rhs=W[:, k, 4 * D + half * 512:4 * D + (half + 1) * 512], start=(k == 0), stop=(k == 7)),
                        reads=[rxT, rW[8 + half]], excl=[rb])
            for half in range(2):
                bk, rb = bks[half]
                S.op("dve", lambda e, bk=bk, half=half: e.bn_stats(
                    out=small[0:P, ST6 + 6 * half:ST6 + 6 * half + 6], in_=bk[0:P, :]),
                    excl=[rb], writes=[smr("st6")])
            S.op("dve", lambda e: e.bn_aggr(out=small[0:P, MV:MV + 2], in_=small[0:P, ST6:ST6 + 12]),
                 reads=[smr("st6")], writes=[smr("mv")])
            S.op("dve", lambda e: e.tensor_scalar(out=small[0:P, MV + 1:MV + 2], in0=small[0:P, MV + 1:MV + 2],
                                                  scalar1=EPS, scalar2=None, op0=ALU.add),
                 reads=[smr("mv")], writes=[smr("mv")])
            pool_pow(MV + 1, VR, 1, P, smr("mv"), smr("vr"))
            S.op("dve", lambda e: e.scalar_tensor_tensor(out=small[0:P, VN:VN + 1], in0=small[0:P, MV:MV + 1],
                                                         scalar=-1.0, in1=small[0:P, VR:VR + 1],
                                                         op0=ALU.mult, op1=ALU.mult),
                 reads=[smr("mv"), smr("vr")], writes=[smr("vn")])
            for half in range(2):
                bk, rb = bks[half]
                S.op("act", lambda e, bk=bk, half=half: e.activation(
                    out=nrm[0:P, t, half * 512:(half + 1) * 512], in_=bk[0:P, :], func=AF.Identity,
                    scale=small[0:P, VR:VR + 1], bias=small[0:P, VN:VN + 1]),
                    reads=[smr("vr"), smr("vn")], excl=[rb], writes=[rnrm[t]])
                if raw_dst is not None:
                    S.op("act", lambda e, bk=bk, half=half: e.activation(
                        out=vraw[0:P, half * 512:(half + 1) * 512], in_=bk[0:P, :], func=AF.Copy),
                        excl=[rb], writes=[rvraw])
            if raw_dst is not None:
                dma_out(raw_dst, vraw[0:P, :], rvraw, "o_v")

        def head_pe(h, n, ntile):
            ui = h % 2
            bk, rb = nb()
            zT_group(bk, 0, 5 * D + h * 128, n, rb)
            zT_group(bk, 256, 3 * D + h * 128, n, rb)
            S.op("act", lambda e: e.activation(out=sgb2[ui][:, 0:n], in_=bk[:, 0:n], func=AF.Silu),
                 excl=[rb], writes=[rsgb2[ui]])
            gk, grb = None, None
            if ntile > 0:
                gk, grb = nb()
                for t in range(ntile):
                    S.op("pe", lambda e, t=t: e.matmul(
                        out=gk[:, t * 128:(t + 1) * 128], lhsT=nrm[:, t, h * 128:(h + 1) * 128], rhs=wsT[:, h, :],
                        start=True, stop=True, skip_group_check=True),
                        reads=[rnrm[t], rC["wsT"]], excl=[grb])
            return (bk, rb, gk, grb)

        def head_dve(h, n, ntile, hb):
            mixT = cx["mixT"]
            rmix = cx["rmix"]
            bk, rb, gk, grb = hb
            ui = h % 2
            S.op("dve", lambda e: e.tensor_tensor(out=ug2[ui][:, 0:n], in0=bk[:, 256:256 + n], in1=sgb2[ui][:, 0:n],
                                                  op=ALU.mult),
                 reads=[rsgb2[ui]], excl=[rb], writes=[rug2[ui]])
            if ntile > 0:
                S.op("dve", lambda e: e.scalar_tensor_tensor(
                    out=tmpa[:, 0:n].rearrange("p (a b) -> p a b", b=128),
                    in0=gk[:, 0:n].rearrange("p (a b) -> p a b", b=128), scalar=blg[:, h:h + 1],
                    in1=Cm[:, h, :].unsqueeze(1).broadcast_to([128, ntile, 128]), op0=ALU.mult, op1=ALU.add),
                    reads=[rC["blg"], rC["Cm"]], excl=[grb], writes=[rtmpa])
                S.op("dve", lambda e: e.tensor_tensor(out=mixT[:, 8 + h, 0:n], in0=tmpa[:, 0:n], in1=ug2[ui][:, 0:n],
                                                      op=ALU.mult),
                     reads=[rtmpa, rug2[ui]], writes=[rmix[8 + h]])

        def conv_and_heads(c0, h0, n, ntile):
            cs = (c0, c0 + 1) if c0 is not None else ()
            hs = (h0, h0 + 1) if h0 is not None else ()
            bks = [conv_pe(c, n) for c in cs]
            hbs = [head_pe(h, n, ntile) for h in hs]
            if cs:
                conv_dve_pair(cs, bks, n)
            for h, hb in zip(hs, hbs):
                head_dve(h, n, ntile, hb)

        def v_tiles(ntile, last_block):
            for t in range(ntile):
                v_tile(t, raw_dst=(cvp_d if (last_block and t == ntile - 1) else None))

        ycnt = [0]

        def out_proj_tiles(tiles):
            mixT = cx["mixT"]
            rmix = cx["rmix"]
            allb = []
            for (t, dst_rows, P, xres_src, xres_row0) in tiles:
                allb.append([nb(), nb()])
            for phase in range(2):
                ks = list(range(8, 16)) if phase == 0 else list(range(8))
                for ti, (t, dst_rows, P, xres_src, xres_row0) in enumerate(tiles):
                    for half in range(2):
                        bk, rb = allb[ti][half]
                        for k in ks:
                            S.op("pe", lambda e, bk=bk, half=half, k=k, t=t, P=P: e.matmul(
                                out=bk[0:P, :], lhsT=mixT[:, k, t * 128:t * 128 + P],
                                rhs=WO[:, k, half * 512:(half + 1) * 512], start=(k == 8), stop=(k == 7)),
                                reads=[rmix[k], rWO[k // 4]], excl=[rb])
            for ti, (t, dst_rows, P, xres_src, xres_row0) in enumerate(tiles):
                yi = ycnt[0] % 2
                ycnt[0] += 1
                ybuf, ry = yb[yi], ryb[yi]
                S.op("pool", lambda e, ybuf=ybuf, P=P, xres_src=xres_src, xres_row0=xres_row0: e.dma_start(
                    out=ybuf[0:P, :], in_=xres_src[xres_row0:xres_row0 + P, :]),
                    writes=[ry], dma="yb%d" % yi)
                for half in range(2):
                    bk, rb = allb[ti][half]
                    S.op("dve", lambda e, bk=bk, half=half, ybuf=ybuf, P=P: e.tensor_tensor(
                        out=ybuf[0:P, half * 512:(half + 1) * 512], in0=bk[0:P, :],
                        in1=ybuf[0:P, half * 512:(half + 1) * 512], op=ALU.add),
                        reads=[ry], excl=[rb], writes=[ry])
                S.op("act", lambda e, ybuf=ybuf, P=P: e.activation(out=xh[0:P, :], in_=ybuf[0:P, :], func=AF.Square,
                                                                   accum_out=small[0:P, YS:YS + 1]),
                     reads=[ry], writes=[smr("ys"), rxh])
                S.op("dve", lambda e, P=P: e.tensor_scalar(out=small[0:P, YS + 1:YS + 2], in0=small[0:P, YS:YS + 1],
                                                           scalar1=1.0 / D, scalar2=EPS, op0=ALU.mult, op1=ALU.add),
                     reads=[smr("ys")], writes=[smr("ym")])
                pool_pow(YS + 1, YS + 2, 1, P, smr("ym"), smr("yr"))
                S.op("dve", lambda e, ybuf=ybuf, P=P: e.scalar_tensor_tensor(
                    out=ybuf[0:P, :], in0=ybuf[0:P, :], scalar=small[0:P, YS + 2:YS + 3], in1=fgB[0:P, :],
                    op0=ALU.mult, op1=ALU.mult),
                    reads=[ry, smr("yr"), rC["fgB"]], writes=[ry])
                r = Res("o")
                S.op("pool", lambda e, dst_rows=dst_rows, ybuf=ybuf, P=P: e.dma_start(out=dst_rows, in_=ybuf[0:P, :]),
                     reads=[ry], writes=[r], dma="yo%d" % yi)
                out_res.append(r)

        NT = NB // 128

        def fe_a(bi, tile):
            if bi < NBLK:
                load_x(bi * NB + tile * 128)
                front_end_a()

        def fe_b(bi, tile):
            if bi < NBLK:
                front_end_b(tile * 128)

        def fe_stats(bi, tile):
            if bi < NBLK:
                load_x(bi * NB + tile * 128)
                par = bi % 2
                front_end_stats(junk=sg2[par][:].rearrange("p a b -> p (a b)"), rjunk=rsg2[par])

        def fe_copy(bi, tile):
            if bi < NBLK:
                front_end_copy()

        n = NS
        st3 = st_d.rearrange("(s k) c -> s k c", k=30)
        css3 = css_d.rearrange("(s k) c -> s k c", k=30)

        def sample_ctx(fn):
            cx.update(cx_sample)
            fn()
            cx.update(cx_prompt)

        def stage_s1():
            rshift = Res("o")
            S.op("sp", lambda e: e.dma_start(out=css3[:, 0:29, :], in_=st3[:, 1:30, :]), writes=[rshift],
                 dma="o_shift")
            out_res.append(rshift)
            for _ in stage_s1_gen():
                pass

        def stage_s1_gen():
            load_x(0, nrows=NS, src=xs_d)
            front_end_a(P=NS)
            yield
            front_end_b(0, P=NS, ntok=NS)
            yield
            a_proj_chunks(range(8), n)
            a_gate_chunks(n, 0)
            yield
            a_in_rows_out(30, n, css3[:, 29, :])

        def stage_s2():
            mixT_, rmix_ = cx["mixT"], cx["rmix"]
            v_tile(0, P=n, raw_dst=cvs_d)
            S.op("act", lambda e: e.activation(out=yb[0][0:n, :], in_=nrm[0:n, 0, :], func=AF.Copy),
                 reads=[rnrm[0]], writes=[ryb[0]])
            bk, rb = nb()
            for c in range(8):
                S.op("pe", lambda e, bk=bk, c=c: e.transpose(out=bk[:, c * 16:c * 16 + n],
                                                             in_=yb[0][0:n, c * 128:(c + 1) * 128],
                                                             identity=ident[0:n, 0:n]),
                     reads=[ryb[0], rC["ident"]], excl=[rb])
            S.op("act", lambda e, bk=bk: e.activation(out=tmpb[0][:, 0:8 * n], in_=bk[:, 0:8 * n], func=AF.Copy),
                 excl=[rb], writes=[rtb])
            for h in range(8):
                hb = head_pe(h, n, 0)
                head_dve(h, n, 0, hb)
                ui = h % 2
                S.op("dve", lambda e, h=h: e.tensor_scalar(out=tmpa[:, 0:n], in0=tmpb[0][:, h * n:(h + 1) * n],
                                                           scalar1=coefA[:, h:h + 1], scalar2=coefB[:, h:h + 1],
                                                           op0=ALU.mult, op1=ALU.add),
                     reads=[rtb, rC["coefA"], rC["coefB"]], writes=[rtmpa])
                S.op("dve", lambda e, h=h, ui=ui: e.tensor_tensor(out=mixT_[:, 8 + h, 0:n], in0=tmpa[:, 0:n],
                                                                  in1=ug2[ui][:, 0:n], op=ALU.mult),
                     reads=[rtmpa, rug2[ui]], writes=[rmix_[8 + h]])

        def stage_s3_gen():
            a2_, ra2_, acv_, racv_ = cx_sample["a2"], cx_sample["ra2"], cx_sample["acv"], cx_sample["racv"]
            dma_in(xh[0:120, :], cw30_d, rxh, "cw30")
            prod = yb[0][:].bitcast(BF16)
            rprod = ryb[0]
            for g in range(4):
                S.op("sp", lambda e, g=g: e.dma_start(out=xf[0:120, :], in_=st_d[g * 120:(g + 1) * 120, :]),
                     writes=[rxf], dma="xf")
                S.op("dve", lambda e: e.tensor_tensor(out=prod[0:120, 0:D], in0=xf[0:120, :], in1=xh[0:120, :],
                                                      op=ALU.mult),
                     reads=[rxf, rxh], writes=[rprod])
                yield
                bkc, rbc = nb()
                for c in range(8):
                    S.op("pe", lambda e, bkc=bkc, c=c: e.matmul(
                        out=bkc[:, c * 4:c * 4 + 4], lhsT=prod[0:120, c * 128:(c + 1) * 128],
                        rhs=sel[0:120, :], start=True, stop=True, skip_group_check=True),
                        reads=[rprod, rC["sel"]], excl=[rbc])
                S.op("dve", lambda e, bkc=bkc, g=g: e.tensor_copy(out=tmpa[:, g * 32:(g + 1) * 32], in_=bkc[:, 0:32]),
                     excl=[rbc], writes=[rtmpa])
            for c in range(8):
                ai = c % 2
                S.op("dve", lambda e, c=c, ai=ai: e.tensor_scalar(
                    out=acc[ai][:, 0:n].rearrange("p (g j) -> p g j", j=4),
                    in0=tmpa[:, 0:128].rearrange("p (g c j) -> p g c j", g=4, j=4)[:, :, c, :],
                    scalar1=cb[:, c:c + 1], scalar2=None,
                    op0=ALU.add), reads=[rC["cb"], rtmpa], writes=[racc[ai]])
                S.op("dve", lambda e, c=c, ai=ai: e.scalar_tensor_tensor(
                    out=acv_[:, c, 0:n], in0=a2_[:, c, 30:30 + n], scalar=wh[:, c * CW + 30:c * CW + 31],
                    in1=acc[ai][:, 0:n], op0=ALU.mult, op1=ALU.add),
                    reads=[ra2_[c], rC["wh"], racc[ai]], writes=[racv_[c]])

        def stage_s3():
            for _ in stage_s3_gen():
                pass

        def stage_s5():
            out_proj_tiles([(0, ys_d[0:n, :], n, xs_d, 0)])

        sample_stages = ["s1", "s3", "s2", "s4", "s5"]

        def emit_sample_stage(prompt_aproj=None, prompt_aproj_parts=None):
            if not sample_stages:
                return False
            st = sample_stages.pop(0)
            used = [False]
            if st == "s1":
                if prompt_aproj_parts is not None:
                    cx.update(cx_sample)
                    rshift = Res("o")
                    S.op("sp", lambda e: e.dma_start(out=css3[:, 0:29, :], in_=st3[:, 1:30, :]), writes=[rshift],
                         dma="o_shift")
                    out_res.append(rshift)
                    gen = stage_s1_gen()
                    for part in prompt_aproj_parts:
                        next(gen, None)
                        cx.update(cx_prompt)
                        part()
                        cx.update(cx_sample)
                    for _ in gen:
                        pass
                    cx.update(cx_prompt)
                    used[0] = True
                else:
                    sample_ctx(stage_s1)
            elif st == "s2":
                sample_ctx(stage_s2)
            elif st == "s3":
                if prompt_aproj_parts is not None:
                    gen = stage_s3_gen()
                    for part in prompt_aproj_parts:
                        next(gen, None)
                        part()
                    for _ in gen:
                        pass
                    used[0] = True
                else:
                    stage_s3()
            elif st == "s4":
                def mid():
                    if prompt_aproj is not None:
                        cx.update(cx_prompt)
                        prompt_aproj()
                        cx.update(cx_sample)
                        used[0] = True
                sample_ctx(lambda: a_layernorm(n, 1, 0, P=n, mid=mid))
            elif st == "s5":
                sample_ctx(stage_s5)
            return used[0]

        def build_diag_scratch():
            for c in range(8):
                q = c % 4
                dst = WO[:, q * 4:(q + 1) * 4, :].rearrange("p a b -> p (a b)")
                nk = CW - TD
                S.op("dve", lambda e, c=c, dst=dst, nk=nk: e.tensor_tensor(
                    out=dst[:, 0:nk * 128].rearrange("p (k j) -> p k j", j=128),
                    in0=identb[:].unsqueeze(1).broadcast_to([128, nk, 128]),
                    in1=wh[:, c * CW + TD:(c + 1) * CW].unsqueeze(2).broadcast_to([128, nk, 128]), op=ALU.mult),
                    reads=[rC["identb"], rC["wh"]], writes=[rWO[q]])
                S.op("sp", lambda e, c=c, dst=dst, nk=nk: e.dma_start(out=dg[c, :, 0:nk * 128], in_=dst[:, 0:nk * 128]),
                     reads=[rWO[q]], writes=[rdg[c]], dma="dgw")

        for tile in range(NT):
            fe_a(0, tile)
            fe_b(0, tile)
        build_diag_scratch()
        a_proj_chunks(range(8), NB)
        if NBLK == 1:
            a_in_rows_out(NB, 30, csp_d)
        a_gate_chunks(NB, 0)
        v_tiles(NT, NBLK == 1)
        load_wout()
        for b in range(NBLK):
            nxt = b + 1 < NBLK
            fe_a(b + 1, 0)
            conv_and_heads(0, 0, NB, NT)
            conv_and_heads(2, 2, NB, NT)
            fe_stats(b + 1, 1)
            conv_and_heads(4, 4, NB, NT)
            conv_and_heads(None, 6, NB, NT)
            fe_b(b + 1, 0)
            fe_copy(b + 1, 1)
            conv_and_heads(6, None, NB, NT)
            fe_b(b + 1, 1)

            def next_aproj(b=b, nxt=nxt):
                if nxt:
                    for pr in range(4):
                        cs = (2 * pr, 2 * pr + 1)
                        halo_shift(NB, cs)
                        a_proj_chunks(cs, NB)
                    if b + 1 == NBLK - 1:
                        a_in_rows_out(NB, 30, csp_d)

            def aproj_part(pr, b=b, nxt=nxt):
                def f():
                    if nxt:
                        cs = (2 * pr, 2 * pr + 1)
                        halo_shift(NB, cs)
                        a_proj_chunks(cs, NB)
                        if pr == 3 and b + 1 == NBLK - 1:
                            a_in_rows_out(NB, 30, csp_d)
                return f

            consumed = False
            if b >= 1:
                consumed = emit_sample_stage(next_aproj, [aproj_part(pr) for pr in range(4)] if nxt else None)
            if not consumed:
                next_aproj()

            def mid(b=b, nxt=nxt):
                if nxt:
                    a_gate_chunks(NB, (b + 1) % 2)
                    v_tiles(NT, b + 1 == NBLK - 1)

            a_layernorm(NB, NT, b % 2, mid=mid)
            out_proj_tiles([(t, y_d[b * NB + t * 128:b * NB + (t + 1) * 128, :], 128, x_d, b * NB + t * 128)
                            for t in range(NT)])

        while sample_stages:
            emit_sample_stage(None)

        S.op("sp", None, reads=out_res)
        S.run()
    return nc


def _host_layout(inp):
    f = np.float32

    def pc(v):
        return np.ascontiguousarray(np.asarray(v, f).reshape(8, 128).T)

    cw = np.asarray(inp["conv_w"], f)[0]
    ws = np.asarray(inp["w_s"], f)[0]
    bs = np.asarray(inp["b_s"], f)[0]
    d = {
        "ng": pc(inp["norm_g"][0]),
        "cw": np.ascontiguousarray(cw.reshape(CW, 8, 128).transpose(2, 1, 0).reshape(128, 8 * CW)),
        "cb": pc(inp["conv_b"][0]),
        "alg": pc(inp["a_ln_g"][0]),
        "alb": pc(inp["a_ln_b"][0]),
        "blg": pc(inp["b_ln_g"][0]),
        "blb": pc(inp["b_ln_b"][0]),
        "ws": np.ascontiguousarray(ws.transpose(1, 0, 2).reshape(128, 1024)),
        "bsB": np.ascontiguousarray(np.broadcast_to(bs.reshape(1, 1024), (128, 1024))),
        "fgB": np.ascontiguousarray(np.broadcast_to(np.asarray(inp["final_g"], f).reshape(1, D), (128, D))),
        "ws00": np.ascontiguousarray(np.broadcast_to(ws[:, 0, 0].reshape(1, 8), (128, 8))),
        "bs0": np.ascontiguousarray(np.broadcast_to(bs[:, 0].reshape(1, 8), (128, 8))),
        "cw30": np.ascontiguousarray(np.tile(cw[0:30], (4, 1))),
        "cb8": np.ascontiguousarray(np.asarray(inp["conv_b"], f)[0].reshape(8, 128)),
        "w_in": np.ascontiguousarray(np.asarray(inp["w_in"], f)[0]),
        "w_out": np.ascontiguousarray(np.asarray(inp["w_out"], f)[0]),
    }
    return d


_PROG = {}


def kernel(**inputs):
    xp = np.asarray(inputs["x_prompt"], np.float32)
    xs = np.asarray(inputs["x_sample"], np.float32)
    st = np.asarray(inputs["state_conv"], np.float32)
    B, SEQ, _ = xp.shape
    NSALL = xs.shape[0]
    NS = NSALL // N_CORES
    shared = _host_layout(inputs)
    key = (SEQ, NS)
    if key not in _PROG:
        _PROG[key] = build_program(SEQ, NS)
    nc = _PROG[key]
    in_maps = []
    for i in range(N_CORES):
        m = dict(shared)
        m["x"] = np.ascontiguousarray(xp[i])
        m["xs"] = np.ascontiguousarray(xs[i * NS:(i + 1) * NS, 0, :])
        m["st"] = np.ascontiguousarray(st[0, i * NS:(i + 1) * NS].reshape(NS * 30, D))
        in_maps.append(m)
    res = run_bass_kernel_spmd(nc, in_maps, core_ids=list(range(N_CORES)))
    r = res.results
    y_prompt = np.stack([r[i]["y"] for i in range(N_CORES)], 0)
    y_sample = np.concatenate([r[i]["ys"] for i in range(N_CORES)], 0).reshape(NSALL, 1, D)
    csp = np.stack([r[i]["csp"] for i in range(N_CORES)], 0)[None]
    css = np.concatenate([r[i]["css"].reshape(NS, 30, D) for i in range(N_CORES)], 0)[None]
    cvp = np.stack([r[i]["cvp"] for i in range(N_CORES)], 0)[None]
    cvs = np.concatenate([r[i]["cvs"] for i in range(N_CORES)], 0).reshape(NSALL, 1, D)[None]
    return (y_prompt.astype(np.float32), y_sample.astype(np.float32), csp.astype(np.float32),
            css.astype(np.float32), cvp.astype(np.float32), cvs.astype(np.float32))
```
